# Optimizing a Trainium2 kernel written in Bass

```python
import math
import jax, jax.numpy as jnp
from jax import lax
import numpy as np

D_MODEL = 2048
BATCH = 4
SEQ = 2048
DEPTH = 4
DEC_BATCH = 32
DEC_SEQ = 8
PAST_LEN = 16384
PAGE_SIZE = 128

N_EVEN = (DEPTH + 1) // 2
N_ODD = DEPTH // 2
D_FF = ((8 * D_MODEL // 3 + 127) // 128) * 128
D_CONV = D_MODEL // 2
CONV_K = 3
D_POOL = D_MODEL // 2
POOL_WINDOWS = (2, 4, 8, 16)
N_POOL_GROUPS = len(POOL_WINDOWS)
POOL_GW = D_POOL // N_POOL_GROUPS
POOL_CTX = max(POOL_WINDOWS) - 1
HEAD_DIM = 64
N_HEADS = D_MODEL // HEAD_DIM
N_KV_HEADS = N_HEADS // 8
GQ = N_HEADS // N_KV_HEADS
WINDOW = 128
BLOCK = WINDOW
KV_BUF = min(WINDOW, PAST_LEN)
ATTN_SCALE = HEAD_DIM ** -0.5
N_BUCKETS = 32
MAX_DISTANCE = 128
EPS = 1e-6
NEG = -1e30

kernel_name = "hybrid_conv_pool_swa_macaron_step"


def rms_norm(x, g):
    xf = x.astype(jnp.float32)
    y = xf * lax.rsqrt(jnp.mean(xf * xf, axis=-1, keepdims=True) + EPS)
    return (y * g.astype(jnp.float32)).astype(x.dtype)


def swiglu(x, w_in, w_out):
    gate, up = jnp.split(x @ w_in, 2, axis=-1)
    return (jax.nn.silu(gate) * up) @ w_out


def short_conv(v, ctx, w):
    L = v.shape[1]
    ext = jnp.concatenate([ctx.astype(v.dtype), v], axis=1)
    y = w[0] * ext[:, 0:L]
    for k in range(1, CONV_K):
        y = y + w[k] * ext[:, k:k + L]
    return y, ext[:, -(CONV_K - 1):]


def pool_mix(u, ctx, p0, w_grp, scale):
    N, L, _ = u.shape
    ext = jnp.concatenate([ctx.astype(u.dtype), u], axis=1)
    extf = ext.astype(jnp.float32)
    cs = jnp.concatenate([jnp.zeros((N, 1, D_POOL), jnp.float32), jnp.cumsum(extf, axis=1)], axis=1)
    end = cs[:, POOL_CTX + 1:POOL_CTX + 1 + L]
    pos = p0 + jnp.arange(L)
    uf = u.astype(jnp.float32)
    outs = []
    for gi, w in enumerate(POOL_WINDOWS):
        c0, c1 = gi * POOL_GW, (gi + 1) * POOL_GW
        start = cs[:, POOL_CTX + 1 - w:POOL_CTX + 1 - w + L, c0:c1]
        cnt = jnp.minimum(w, pos + 1).astype(jnp.float32)[None, :, None]
        outs.append((end[..., c0:c1] - start) / cnt - uf[..., c0:c1])
    d = jnp.stack(outs, axis=2).astype(u.dtype)
    y = jnp.einsum('nlgc,gcd->nlgd', d, w_grp).reshape(N, L, D_POOL)
    return y * scale, ext[:, -POOL_CTX:]


def t5_bucket(dist):
    max_exact = N_BUCKETS // 2
    n = jnp.maximum(dist, 0)
    ratio = jnp.log(jnp.maximum(n, 1).astype(jnp.float32) / max_exact) / math.log(MAX_DISTANCE / max_exact)
    large = jnp.minimum(max_exact + (ratio * (N_BUCKETS - max_exact)).astype(jnp.int32), N_BUCKETS - 1)
    return jnp.where(n < max_exact, n, large)


def sink_attend(q, k, v, dist, valid, sinks, rel_bias):
    Qn, Sn = dist.shape
    s = jnp.einsum('nbqkgd,nbskd->nbkgqs', q, k, preferred_element_type=jnp.float32) * ATTN_SCALE
    bias = rel_bias[t5_bucket(dist)].astype(jnp.float32)
    s = s + bias.transpose(2, 0, 1).reshape(N_KV_HEADS, GQ, Qn, Sn)
    s = jnp.where(valid[None, :, None, None], s, NEG)
    sink = sinks.astype(jnp.float32).reshape(N_KV_HEADS, GQ)[:, :, None, None]
    m = jnp.maximum(jnp.max(s, axis=-1, keepdims=True), sink)
    p = jnp.exp(s - m)
    p = p / (jnp.sum(p, axis=-1, keepdims=True) + jnp.exp(sink - m))
    return jnp.einsum('nbkgqs,nbskd->nbqkgd', p.astype(v.dtype), v)


def attn_prompt(q, k, v, sinks, rel_bias):
    N, L = q.shape[:2]
    nb = L // BLOCK
    qb = q.reshape(N, nb, BLOCK, N_KV_HEADS, GQ, HEAD_DIM)

    def band(t):
        tb = t.reshape(N, nb, BLOCK, N_KV_HEADS, HEAD_DIM)
        prev = jnp.concatenate([jnp.zeros_like(tb[:, :1]), tb[:, :-1]], axis=1)
        return jnp.concatenate([prev, tb], axis=2)

    qi = jnp.arange(BLOCK)[:, None]
    sj = jnp.arange(2 * BLOCK)[None, :]
    dist = BLOCK + qi - sj
    blk = jnp.arange(nb)[:, None, None]
    valid = (dist >= 0) & (dist <= WINDOW) & ((blk - 1) * BLOCK + sj >= 0)
    o = sink_attend(qb, band(k), band(v), dist, valid, sinks, rel_bias)
    return o.reshape(N, L, N_HEADS * HEAD_DIM)


def attn_sample(q, k, v, k_ctx, v_ctx, sinks, rel_bias):
    N, L = q.shape[:2]
    W = k_ctx.shape[1]
    kk = jnp.concatenate([k_ctx.astype(k.dtype), k], axis=1)
    vv = jnp.concatenate([v_ctx.astype(v.dtype), v], axis=1)
    qi = jnp.arange(L)[:, None]
    sj = jnp.arange(W + L)[None, :]
    dist = qi + W - sj
    valid = ((dist >= 0) & (dist <= WINDOW))[None]
    o = sink_attend(q.reshape(N, 1, L, N_KV_HEADS, GQ, HEAD_DIM), kk[:, None], vv[:, None],
                    dist, valid, sinks, rel_bias)
    return o.reshape(N, L, N_HEADS * HEAD_DIM), kk[:, -W:], vv[:, -W:]


def _trunk(x, conv_ctx, pool_ctx, k_ctx, v_ctx, p0, prompt, norm_g, w_ffn_in, w_ffn_out,
           w_mix_in, conv_w, pool_w, pool_scale, w_mix_out, w_qkv, w_o, attn_sinks, rel_bias):
    new_conv, new_pool, new_k, new_v = [], [], [], []
    for i in range(DEPTH):
        g = norm_g[i]
        x = x + 0.5 * rms_norm(swiglu(rms_norm(x, g[0]), w_ffn_in[i, 0], w_ffn_out[i, 0]), g[1])
        h = rms_norm(x, g[2])
        j = i // 2
        if i % 2 == 0:
            z = h @ w_mix_in[j]
            hc, gc, gb, u = jnp.split(z, [D_CONV, 2 * D_CONV, 3 * D_CONV], axis=-1)
            yc, cc = short_conv(gc * hc, conv_ctx[j], conv_w[j])
            yp, pc = pool_mix(u, pool_ctx[j], p0, pool_w[j], pool_scale[j])
            mix = jnp.concatenate([gb * yc, yp], axis=-1) @ w_mix_out[j]
            new_conv.append(cc)
            new_pool.append(pc)
        else:
            n, l = h.shape[:2]
            q, k, v = jnp.split(h @ w_qkv[j], [N_HEADS * HEAD_DIM, (N_HEADS + N_KV_HEADS) * HEAD_DIM], axis=-1)
            k = k.reshape(n, l, N_KV_HEADS, HEAD_DIM)
            v = v.reshape(n, l, N_KV_HEADS, HEAD_DIM)
            if prompt:
                att = attn_prompt(q, k, v, attn_sinks[j], rel_bias)
                kc, vc = k[:, -KV_BUF:], v[:, -KV_BUF:]
            else:
                att, kc, vc = attn_sample(q, k, v, k_ctx[j], v_ctx[j], attn_sinks[j], rel_bias)
            mix = att @ w_o[j]
            new_k.append(kc)
            new_v.append(vc)
        x = x + rms_norm(mix, g[3])
        x = x + 0.5 * rms_norm(swiglu(rms_norm(x, g[4]), w_ffn_in[i, 1], w_ffn_out[i, 1]), g[5])
    return x, jnp.stack(new_conv), jnp.stack(new_pool), jnp.stack(new_k), jnp.stack(new_v)


def setup_inputs(seed: int = 0) -> dict:
    key = jax.random.key(seed)
    ks = jax.random.split(key, 18)

    def nrm(k, shape, scale):
        return jax.random.normal(k, shape, jnp.float32) * scale

    return {
        "x_prompt": nrm(ks[0], (BATCH, SEQ, D_MODEL), 1.0),
        "x_sample": nrm(ks[1], (DEC_BATCH, DEC_SEQ, D_MODEL), 1.0),
        "state_conv": nrm(ks[2], (N_EVEN, DEC_BATCH, CONV_K - 1, D_CONV), 1.0),
        "state_pool": nrm(ks[3], (N_EVEN, DEC_BATCH, POOL_CTX, D_POOL), 1.0),
        "cache_k": nrm(ks[4], (N_ODD, DEC_BATCH, KV_BUF, N_KV_HEADS, HEAD_DIM), 1.0),
        "cache_v": nrm(ks[5], (N_ODD, DEC_BATCH, KV_BUF, N_KV_HEADS, HEAD_DIM), 1.0),
        "norm_g": 1.0 + nrm(ks[6], (DEPTH, 6, D_MODEL), 0.05),
        "w_ffn_in": nrm(ks[7], (DEPTH, 2, D_MODEL, 2 * D_FF), D_MODEL ** -0.5),
        "w_ffn_out": nrm(ks[8], (DEPTH, 2, D_FF, D_MODEL), D_FF ** -0.5),
        "w_mix_in": nrm(ks[9], (N_EVEN, D_MODEL, 3 * D_CONV + D_POOL), D_MODEL ** -0.5),
        "conv_w": nrm(ks[10], (N_EVEN, CONV_K, D_CONV), CONV_K ** -0.5),
        "pool_w": nrm(ks[11], (N_EVEN, N_POOL_GROUPS, POOL_GW, POOL_GW), POOL_GW ** -0.5),
        "pool_scale": 1.0 + nrm(ks[12], (N_EVEN, D_POOL), 0.1),
        "w_mix_out": nrm(ks[13], (N_EVEN, D_CONV + D_POOL, D_MODEL), (D_CONV + D_POOL) ** -0.5),
        "w_qkv": nrm(ks[14], (N_ODD, D_MODEL, (N_HEADS + 2 * N_KV_HEADS) * HEAD_DIM), D_MODEL ** -0.5),
        "w_o": nrm(ks[15], (N_ODD, N_HEADS * HEAD_DIM, D_MODEL), (N_HEADS * HEAD_DIM) ** -0.5),
        "attn_sinks": nrm(ks[16], (N_ODD, N_HEADS), 0.5),
        "rel_bias": nrm(ks[17], (N_BUCKETS, N_HEADS), 0.5),
    }


def reference(x_prompt, x_sample, state_conv, state_pool, cache_k, cache_v, norm_g, w_ffn_in,
              w_ffn_out, w_mix_in, conv_w, pool_w, pool_scale, w_mix_out, w_qkv, w_o,
              attn_sinks, rel_bias):
    nbp = x_prompt.shape[0]
    conv0 = jnp.zeros((N_EVEN, nbp, CONV_K - 1, D_CONV), x_prompt.dtype)
    pool0 = jnp.zeros((N_EVEN, nbp, POOL_CTX, D_POOL), x_prompt.dtype)
    y_prompt, conv_p, pool_p, k_p, v_p = _trunk(
        x_prompt, conv0, pool0, None, None, 0, True, norm_g, w_ffn_in, w_ffn_out,
        w_mix_in, conv_w, pool_w, pool_scale, w_mix_out, w_qkv, w_o, attn_sinks, rel_bias)
    y_sample, conv_s, pool_s, k_s, v_s = _trunk(
        x_sample, state_conv, state_pool, cache_k, cache_v, PAST_LEN, False, norm_g, w_ffn_in,
        w_ffn_out, w_mix_in, conv_w, pool_w, pool_scale, w_mix_out, w_qkv, w_o, attn_sinks, rel_bias)
    return (y_prompt, y_sample, conv_p, pool_p, k_p, v_p, conv_s, pool_s, k_s, v_s)
```

```python
import math
from contextlib import ExitStack

import numpy as np
import concourse.bass as bass
import concourse.mybir as mybir
from concourse.bass_utils import run_bass_kernel_spmd

F32 = mybir.dt.float32
BF16 = mybir.dt.bfloat16
AF = mybir.ActivationFunctionType
ALU = mybir.AluOpType

D = 2048
NCH = 16
DFF = 5504
NFF = 43
DEPTH = 4
R = 1168
NS = 32
T = R + NS
TT = 300
NT = T // TT
TP = 600
HALO0 = 2048 - R
EPS = 1e-6
NEG = -1e30

ENGS = ("pe", "act", "dve", "pool", "sp")


class Op:
    __slots__ = ("eng", "fn", "deps", "sig", "dma_sem", "dma_val", "idx")

    def __init__(self, eng, fn):
        self.eng = eng
        self.fn = fn
        self.deps = []
        self.sig = False
        self.dma_sem = None
        self.dma_val = 0
        self.idx = 0


class Sched:
    def __init__(self):
        self.ops = {e: [] for e in ENGS}
        self.res = {}
        self.dma_cnt = {}
        self.dma_last = {}
        self.all_dma = []

    def _overlaps(self, space, lo, hi):
        lst = self.res.setdefault(space, [])
        return [r for r in lst if r[0] < hi and lo < r[1]]

    def add(self, eng, fn, r=(), w=(), dma=None):
        op = Op(eng, fn)
        deps = {}

        def dep(d, raw):
            if d is None or d is op:
                return
            if d.dma_sem is None and d.eng == eng and not raw:
                return
            deps[id(d)] = d

        rkey = eng if dma is None else ("dma", id(op))
        for (space, lo, hi) in r:
            for rec in self._overlaps(space, lo, hi):
                dep(rec[2], True)
                rec[3][rkey] = op
        for (space, lo, hi) in w:
            lst = self.res.setdefault(space, [])
            new = []
            for rec in lst:
                if rec[0] < hi and lo < rec[1]:
                    dep(rec[2], False)
                    for rd in rec[3].values():
                        dep(rd, False)
                    if rec[0] < lo:
                        new.append([rec[0], lo, rec[2], dict(rec[3])])
                    if hi < rec[1]:
                        new.append([hi, rec[1], rec[2], dict(rec[3])])
                else:
                    new.append(rec)
            new.append([lo, hi, op, {}])
            self.res[space] = new
        if dma is not None:
            prev = self.dma_last.get(dma)
            if prev is not None:
                deps[id(prev)] = prev
            op.dma_sem = dma
            self.dma_cnt[dma] = self.dma_cnt.get(dma, 0) + 16
            op.dma_val = self.dma_cnt[dma]
            self.dma_last[dma] = op
            self.all_dma.append(op)
        op.deps = list(deps.values())
        for d in op.deps:
            d.sig = True
        self.ops[eng].append(op)
        return op

    def emit(self, nc, stack):
        engs = {"pe": nc.tensor, "act": nc.scalar, "dve": nc.vector, "pool": nc.gpsimd, "sp": nc.sync}
        esem = {e: stack.enter_context(nc.semaphore("es_" + e)) for e in ENGS}
        dsem = {n: stack.enter_context(nc.semaphore("ds_" + n)) for n in self.dma_cnt}
        for e in ENGS:
            k = 0
            for op in self.ops[e]:
                if op.dma_sem is None and op.sig:
                    k += 1
                    op.idx = k
        final = [(dsem[n], v) for n, v in self.dma_cnt.items()]
        block = stack.enter_context(nc.Block())
        ops = self.ops

        def make(e):
            def body(eng):
                waited = {}
                for op in ops[e]:
                    for d in op.deps:
                        if d.dma_sem is not None:
                            key, sem, val = "d_" + d.dma_sem, dsem[d.dma_sem], d.dma_val
                        else:
                            key, sem, val = "e_" + d.eng, esem[d.eng], d.idx
                        if waited.get(key, 0) >= val:
                            continue
                        waited[key] = val
                        eng.wait_ge(sem, val)
                    ins = op.fn(eng)
                    if op.dma_sem is not None:
                        ins.then_inc(dsem[op.dma_sem], 16)
                    elif op.sig:
                        ins.then_inc(esem[e], 1)
                if e == "sp":
                    for sem, val in final:
                        eng.wait_ge(sem, val)
            return body

        block.tensor(make("pe"))
        block.scalar(make("act"))
        block.vector(make("dve"))
        block.gpsimd(make("pool"))
        block.sync(make("sp"))


class Reg:
    def __init__(self, space, base_t, off, dt, nch, ncol, tile=None):
        self.space = space
        self.off = off
        self.es = 2 if dt == BF16 else 4
        self.nch = nch
        self.ncol = ncol
        nbytes = nch * ncol * self.es
        assert off % 4 == 0 and nbytes % 4 == 0
        self.nbytes = nbytes
        v = base_t[:, off // 4:(off + nbytes) // 4]
        if dt != F32:
            v = v.bitcast(dt)
        self.v = v.rearrange("p (c n) -> p c n", c=nch)
        self.vt = None
        if tile is not None:
            self.tile = tile
            self.vt = v.rearrange("p (c t n) -> p c t n", c=nch, n=tile)

    def iv(self, c, lo=0, hi=None):
        hi = self.ncol if hi is None else hi
        return (self.space, self.off + (c * self.ncol + lo) * self.es, self.off + (c * self.ncol + hi) * self.es)

    def ivs(self, c0, c1, lo=0, hi=None):
        return [self.iv(c, lo, hi) for c in range(c0, c1)]

    def all(self):
        return (self.space, self.off, self.off + self.nbytes)


class Builder:
    def __init__(self, layers=DEPTH, phases=("ffn0", "mix", "ffn1")):
        self.layers = layers
        self.phases = phases
        self.S = Sched()
        self.bank = 0
        self.reserved = set()
        self.pref = {}

    def pe(self, fn, r=(), w=()):
        return self.S.add("pe", fn, r, w)

    def act(self, fn, r=(), w=()):
        return self.S.add("act", fn, r, w)

    def dve(self, fn, r=(), w=()):
        return self.S.add("dve", fn, r, w)

    def wdma(self, fn, r=(), w=(), sem=None):
        return self.S.add("pool", fn, r, w, dma=sem)

    def dma(self, fn, r=(), w=(), sem=None):
        return self.S.add("sp", fn, r, w, dma=sem)

    def pair(self):
        while True:
            q = self.bank
            self.bank = (self.bank + 1) % 4
            if q not in self.reserved:
                return q

    def ps_iv(self, q, t=None, n=TT):
        base = q * 4096
        if t is None:
            return ("ps", base, base + 4096)
        return ("ps", base + t * 2048, base + t * 2048 + n * 4)

    def ring_slot(self):
        s = self.ring_i
        self.ring_i = (self.ring_i + 1) % self.NSLOT
        return s

    def build(self):
        nc = bass.Bass("TRN2", target_bir_lowering=False)
        self.nc = nc
        L = self.layers
        dr = {}

        def din(name, shape):
            dr[name] = nc.dram_tensor(name, list(shape), F32, kind="ExternalInput").ap()

        def dout(name, shape):
            dr[name] = nc.dram_tensor(name, list(shape), F32, kind="ExternalOutput").ap()

        din("xT", (128, NCH, T))
        din("gT", (128, DEPTH * 6 * NCH))
        din("w_ffn_in", (DEPTH, 2, D, 2 * DFF))
        din("w_ffn_out", (DEPTH, 2, DFF, D))
        din("w_mix_in", (2, D, 4096))
        din("w_mix_out", (2, D, D))
        din("pool_w", (2, 4, 256, 256))
        din("cwT", (128, 48))
        din("pscT", (128, 16))
        din("invc", (128, 60))
        din("sconv", (2, 4, 2, 1024))
        din("spool", (2, 4, 15, 1024))
        din("w_qkv", (2, D, 2560))
        din("w_o", (2, D, D))
        din("sinkB", (128, 64))
        din("ident", (128, 128))
        din("biasP", (32, 128, 256))
        din("biasS", (32, 4, 8, 160))
        din("ck", (2, 4, 128, 256))
        din("cv", (2, 4, 128, 256))
        dout("kp", (2, 128, 256))
        dout("vp", (2, 128, 256))
        dout("ks", (2, 4, 128, 256))
        dout("vs", (2, 4, 128, 256))
        dout("yT", (128, NCH, T))
        dout("convp", (2, 2, 1024))
        dout("poolp", (2, 15, 1024))
        dout("convs", (2, 4, 2, 1024))
        dout("pools", (2, 4, 15, 1024))
        self.dr = dr

        with ExitStack() as st:
            XB = NCH * T * 4
            ARB = 92160
            self.NSLOT = 4
            SLOTB = 5632
            MISCB = 20992
            xt = st.enter_context(nc.sbuf_tensor("xT_sb", [128, XB // 4], F32))
            ar = st.enter_context(nc.sbuf_tensor("arena", [128, ARB // 4], F32))
            rg = st.enter_context(nc.sbuf_tensor("ring", [128, self.NSLOT * SLOTB // 4], F32))
            ms = st.enter_context(nc.sbuf_tensor("misc", [128, MISCB // 4], F32))
            ps = st.enter_context(nc.psum_tensor("ps", [128, 4, 2, 512], F32))
            self.ps = ps
            self.ring_i = 0

            self.X = Reg("x", xt, 0, F32, NCH, T, tile=TT)
            self.A_aT = Reg("ar", ar, 0, BF16, NFF, TP, tile=TT)
            self.A_sq = Reg("ar", ar, 0, BF16, NCH, TP, tile=TT)
            self.A_yT = Reg("ar", ar, 51600, F32, NCH, TP, tile=TT)
            self.A_hT = Reg("ar", ar, 51600, BF16, NCH, TP, tile=TT)
            self.B_hT = Reg("ar", ar, 0, BF16, NCH, T, tile=TT)
            self.B_yT = Reg("ar", ar, 0, F32, NCH, TP, tile=TT)
            self.B_mT = Reg("ar", ar, 38400, BF16, NCH, T, tile=TT)
            self.B_sq = Reg("ar", ar, 76800, BF16, 8, TP, tile=TT)
            self.B_Ev = Reg("ar", ar, 76800, F32, 1, 1212)
            self.B_yc = Reg("ar", ar, 81648, F32, 1, 1212)
            self.B_Eu = Reg("ar", ar, 76800, F32, 1, 1276)
            self.B_Es = Reg("ar", ar, 81904, F32, 1, 1276)
            self.B_dT = Reg("ar", ar, 87008, BF16, 2, T, tile=TT)
            self.slots = [Reg("ring", rg, i * SLOTB, BF16, 22, 128) for i in range(self.NSLOT)]
            o = 0
            self.M_g = Reg("misc", ms, o, F32, 1, DEPTH * 6 * NCH); o += DEPTH * 6 * NCH * 4
            self.M_rs = [Reg("misc", ms, o + k * 2400, F32, 1, TP, tile=TT) for k in range(2)]; o += 4800
            self.M_sg = [Reg("misc", ms, o + k * 1200, BF16, 1, TP, tile=TT) for k in range(2)]; o += 2400
            self.M_ones = Reg("misc", ms, o, BF16, 1, 128); o += 256
            self.M_cst = Reg("misc", ms, o, F32, 1, 8); o += 32
            self.M_cw = Reg("misc", ms, o, F32, 1, 48); o += 192
            self.M_psc = Reg("misc", ms, o, F32, 1, 16); o += 64
            self.M_invc = Reg("misc", ms, o, F32, 1, 60); o += 240
            self.M_t15 = Reg("misc", ms, o, F32, 1, 16); o += 64
            self.M_pw = Reg("misc", ms, o, BF16, 8, 256); o += 4096
            pwo = o - 4096
            self.C_KVf = Reg("misc", ms, pwo, F32, 4, 160)
            self.C_KcT = Reg("misc", ms, pwo + 2560, BF16, 4, 128)
            self.C_Vc = Reg("misc", ms, pwo + 3584, BF16, 1, 256)
            self.M_bias = [Reg("misc", ms, o + k * 1024, F32, 1, 256) for k in range(2)]; o += 2048
            self.M_S = Reg("misc", ms, o, F32, 1, 256); o += 1024
            self.M_P = Reg("misc", ms, o, BF16, 1, 256); o += 512
            self.M_PT = Reg("misc", ms, o, BF16, 2, 128); o += 512
            self.M_sm = Reg("misc", ms, o, F32, 1, 16); o += 64
            self.M_sink = Reg("misc", ms, o, F32, 1, 64); o += 256
            self.M_idb = Reg("misc", ms, o, BF16, 1, 128); o += 256
            self.M_idf = Reg("misc", ms, o, F32, 1, 128); o += 512
            self.M_stg = Reg("misc", ms, o, F32, 1, 512); o += 2048
            so_ = self.M_stg.off
            self.E_so = Reg("misc", ms, so_, F32, 1, 76)
            self.E_st = Reg("misc", ms, so_ + 304, F32, 1, 128)
            self.E_cs = Reg("misc", ms, so_ + 816, F32, 1, 128)
            self.C_hT = Reg("ar", ar, 0, BF16, NCH, T, tile=TT)
            self.C_attT = Reg("ar", ar, 0, BF16, NCH, T, tile=TT)
            self.C_qT = Reg("ar", ar, 38400, BF16, NCH, T, tile=TT)
            self.C_yT = Reg("ar", ar, 38400, F32, NCH, TP, tile=TT)
            self.C_kdT = Reg("ar", ar, 76800, BF16, 4, T, tile=TT)
            self.C_V = Reg("ar", ar, 86400, BF16, 11, 256)
            self.bias_i = 0
            mb = self.M_bias[0].off
            self.at_sets = []
            for k in range(2):
                ro = k * 8192
                self.at_sets.append((Reg("ring", rg, ro, F32, 1, 1024), Reg("ring", rg, ro + 4096, BF16, 1, 1024),
                                     Reg("ring", rg, ro + 6144, BF16, 1, 1024), [self.M_sm, self.M_t15][k]))
            self.at_bias = [Reg("ring", rg, 16384, F32, 1, 1024), Reg("misc", ms, mb, F32, 1, 1024)]
            self.at_i = 0
            self.at_diag = [Reg("ring", rg, 20480 + k * 1024, BF16, 1, 512) for k in range(2)]
            self.at_P3 = [self.at_sets[0][1], self.at_sets[1][1], Reg("misc", ms, pwo, BF16, 1, 1024)]
            self.at_p3 = 0
            assert o <= MISCB, o
            self.rs_i = 0
            self.sg_i = 0

            self.prologue()
            for i in range(L):
                if "ffn0" in self.phases:
                    self.ffn(i, 0)
                if "mix" in self.phases:
                    if i % 2 == 0:
                        self.mix_even(i)
                    else:
                        self.mix_odd(i)
                if "ffn1" in self.phases:
                    self.ffn(i, 1)
            self.epilogue()
            self.S.emit(nc, st)
        return nc

    def prologue(self):
        dr = self.dr
        X = self.X
        for c in range(NCH):
            self.dma(lambda e, c=c: e.dma_start(out=X.v[:, c, :], in_=dr["xT"][:, c, :]),
                     w=[X.iv(c)], sem="in%d" % (c % 4))
        for (reg, name) in ((self.M_g, "gT"), (self.M_cw, "cwT"), (self.M_psc, "pscT"), (self.M_invc, "invc"), (self.M_sink, "sinkB"), (self.M_idf, "ident")):
            self.dma(lambda e, reg=reg, name=name: e.dma_start(out=reg.v[:, 0, :], in_=dr[name][:, :]),
                     w=[reg.all()], sem="cst_" + name)
        O = self.M_ones
        self.dve(lambda e: e.memset(O.v[:, 0, :], 1.0), w=[O.all()])
        IB, IF = self.M_idb, self.M_idf
        self.dve(lambda e: e.tensor_copy(out=IB.v[:, 0, :], in_=IF.v[:, 0, :]), r=[IF.all()], w=[IB.all()])
        C = self.M_cst
        self.dve(lambda e: e.memset(C.v[:, 0, 0:1], EPS), w=[C.iv(0, 0, 1)])
        self.dve(lambda e: e.memset(C.v[:, 0, 1:2], 4.0 * EPS), w=[C.iv(0, 1, 2)])

    def epilogue(self):
        dr = self.dr
        X = self.X
        for c in range(NCH):
            self.dma(lambda e, c=c: e.dma_start(out=dr["yT"][:, c, :], in_=X.v[:, c, :]),
                     r=[X.iv(c)], sem="out%d" % (c % 4))

    def stats(self, src, tiles, scale, bias_col, SQ):
        ps, C, O = self.ps, self.M_cst, self.M_ones
        rs = self.M_rs[self.rs_i]
        self.rs_i ^= 1
        G = SQ.nch
        nt = len(tiles)
        q = self.pair()
        for g0 in range(0, NCH, G):
            for cc in range(G):
                c = g0 + cc
                for t, st_ in enumerate(tiles):
                    self.act(lambda e, o=SQ.vt[:, cc, t, :], a=src.vt[:, c, st_, :]: e.activation(out=o, in_=a, func=AF.Square),
                             r=[src.iv(c, st_ * TT, st_ * TT + TT)], w=[SQ.iv(cc, t * TT, t * TT + TT)])
            for t in range(nt):
                for cc in range(G):
                    c = g0 + cc
                    self.pe(lambda e, o=ps[:, q, t, 0:TT], b=SQ.vt[:, cc, t, :], c=c: e.matmul(
                        o, lhsT=O.v[:, 0, :], rhs=b, start=(c == 0), stop=(c == NCH - 1)),
                        r=[O.all(), SQ.iv(cc, t * TT, t * TT + TT)], w=[self.ps_iv(q, t)])
        for t in range(nt):
            self.act(lambda e, o=rs.vt[:, 0, t, :], a=ps[:, q, t, 0:TT]: e.activation(
                out=o, in_=a, func=AF.Sqrt, scale=scale, bias=C.v[:, 0, bias_col:bias_col + 1]),
                r=[self.ps_iv(q, t), C.all()], w=[rs.iv(0, t * TT, t * TT + TT)])
            self.dve(lambda e, o=rs.vt[:, 0, t, :]: e.reciprocal(out=o, in_=o),
                     r=[rs.iv(0, t * TT, t * TT + TT)], w=[rs.iv(0, t * TT, t * TT + TT)])
        return rs

    def gcol(self, layer, k, c):
        return (layer * 6 + k) * NCH + c

    def prenorm(self, i, k, hT, h_t0, x_t0, SQ, resid=None):
        X, G = self.X, self.M_g
        if x_t0 in self.pref:
            rs = self.pref.pop(x_t0)
        else:
            self.rs_i = 0
            rs = self.stats(X, [x_t0, x_t0 + 1], 1.0 / D, 0, SQ)
        for c in range(NCH):
            if resid is not None and c % 2 == 0:
                resid[c // 2]()
            gc = self.gcol(i, k, c)
            self.dve(lambda e, o=hT.vt[:, c, h_t0:h_t0 + 2, :], a=X.vt[:, c, x_t0:x_t0 + 2, :], g=G.v[:, 0, gc:gc + 1],
                     b=rs.vt[:, 0, :, :]: e.scalar_tensor_tensor(out=o, in0=a, scalar=g, in1=b, op0=ALU.mult, op1=ALU.mult),
                     r=[X.iv(c, x_t0 * TT, x_t0 * TT + TP), G.all(), rs.all()], w=[hT.iv(c, h_t0 * TT, h_t0 * TT + TP)])
        if resid is not None:
            for c in range(NCH // 2, NCH):
                resid[c]()

    def out_proj(self, wsrc, nk, inT, in_t0, p, i, kpost, half, yT, SQ, hi_space=None, pref_next=False, defer=False):
        X, G, ps, C, O = self.X, self.M_g, self.ps, self.M_cst, self.M_ones
        t0 = 2 * p
        nx_t0 = 2 if p == 0 else 0
        do_pref = pref_next
        pieces = [(0, 22), (22, 43)] if nk == 43 else [(0, nk)]
        qy = self.pair()
        self.reserved.add(qy)
        qx = None
        if do_pref:
            qx = self.pair()
            self.reserved.add(qx)
        SGy, SGx = self.M_sg

        def statmm(c):
            for t in range(2):
                self.pe(lambda e, o=ps[:, qy, t, 0:TT], b=SGy.vt[:, 0, t, :], c=c: e.matmul(o, lhsT=O.v[:, 0, :], rhs=b, start=(c == 0), stop=(c == NCH - 1)),
                        r=[O.all(), SGy.iv(0, t * TT, t * TT + TT)], w=[self.ps_iv(qy, t)])
            if do_pref:
                for t in range(2):
                    self.pe(lambda e, o=ps[:, qx, t, 0:TT], b=SGx.vt[:, 0, t, :], c=c: e.matmul(o, lhsT=O.v[:, 0, :], rhs=b, start=(c == 0), stop=(c == NCH - 1)),
                            r=[O.all(), SGx.iv(0, t * TT, t * TT + TT)], w=[self.ps_iv(qx, t)])

        for oc in range(NCH):
            sl = []
            for (k0, k1) in pieces:
                k = self.ring_slot()
                SL = self.slots[k]
                self.wdma(lambda e, o=SL.v[:, 0:k1 - k0, :],
                          a=wsrc[k0 * 128:k1 * 128, oc * 128:(oc + 1) * 128].rearrange("(c p) n -> p c n", p=128):
                          e.dma_start(out=o, in_=a), w=[SL.all()], sem="ring%d" % k)
                sl.append((SL, k0, k1))
            q = self.pair()
            for t in range(2):
                for (SL, k0, k1) in sl:
                    for kc in range(k0, k1):
                        self.pe(lambda e, o=ps[:, q, t, 0:TT], a=SL.v[:, kc - k0, :], b=inT.vt[:, kc, in_t0 + t, :], kc=kc: e.matmul(
                            o, lhsT=a, rhs=b, start=(kc == 0), stop=(kc == nk - 1)),
                            r=[SL.iv(kc - k0), inT.iv(kc, (in_t0 + t) * TT, (in_t0 + t + 1) * TT)], w=[self.ps_iv(q, t)])
            if oc >= 1:
                statmm(oc - 1)
            self.act(lambda e, o=yT.vt[:, oc, :, :], a=ps[:, q, :, 0:TT]: e.activation(out=o, in_=a, func=AF.Copy),
                     r=[self.ps_iv(q)], w=[yT.iv(oc)])
            self.act(lambda e, o=SGy.vt[:, 0, :, :], a=yT.vt[:, oc, :, :]: e.activation(out=o, in_=a, func=AF.Square),
                     r=[yT.iv(oc)], w=[SGy.all()])
            if do_pref:
                self.act(lambda e, o=SGx.vt[:, 0, :, :], a=X.vt[:, oc, nx_t0:nx_t0 + 2, :]: e.activation(out=o, in_=a, func=AF.Square),
                         r=[X.iv(oc, nx_t0 * TT, nx_t0 * TT + TP)], w=[SGx.all()])
        statmm(NCH - 1)
        rs2 = self.M_rs[1]
        scale, bcol = (4.0 / D, 1) if half else (1.0 / D, 0)
        jobs = [(qy, rs2, scale, bcol)]
        if do_pref:
            jobs.append((qx, self.M_rs[0], 1.0 / D, 0))
        for (qq, rs, sc_, bc_) in jobs:
            for t in range(2):
                self.act(lambda e, o=rs.vt[:, 0, t, :], a=ps[:, qq, t, 0:TT], sc_=sc_, bc_=bc_: e.activation(
                    out=o, in_=a, func=AF.Sqrt, scale=sc_, bias=C.v[:, 0, bc_:bc_ + 1]),
                    r=[self.ps_iv(qq, t), C.all()], w=[rs.iv(0, t * TT, t * TT + TT)])
                self.dve(lambda e, o=rs.vt[:, 0, t, :]: e.reciprocal(out=o, in_=o),
                         r=[rs.iv(0, t * TT, t * TT + TT)], w=[rs.iv(0, t * TT, t * TT + TT)])
        self.reserved.discard(qy)
        if do_pref:
            self.reserved.discard(qx)
            self.pref[nx_t0] = self.M_rs[0]

        def resid_op(c):
            gc = self.gcol(i, kpost, c)
            self.dve(lambda e, o=yT.vt[:, c, :, :], g=G.v[:, 0, gc:gc + 1], b=rs2.vt[:, 0, :, :]: e.scalar_tensor_tensor(
                out=o, in0=o, scalar=g, in1=b, op0=ALU.mult, op1=ALU.mult),
                r=[yT.iv(c), G.all(), rs2.all()], w=[yT.iv(c)])
            self.dve(lambda e, o=X.vt[:, c, t0:t0 + 2, :], b=yT.vt[:, c, :, :]: e.tensor_tensor(out=o, in0=o, in1=b, op=ALU.add),
                     r=[X.iv(c, t0 * TT, t0 * TT + TP), yT.iv(c)], w=[X.iv(c, t0 * TT, t0 * TT + TP)])

        ops = [lambda c=c: resid_op(c) for c in range(NCH)]
        if defer:
            return ops
        for f in ops:
            f()
        return None

    def ffn(self, i, s):
        dr = self.dr
        aT, yT, hT, ps = self.A_aT, self.A_yT, self.A_hT, self.ps
        kpre = 0 if s == 0 else 4
        kpost = 1 if s == 0 else 5
        win = dr["w_ffn_in"][i, s]
        wout = dr["w_ffn_out"][i, s]
        resid = None
        for p in range(2):
            self.prenorm(i, kpre, hT, 0, 2 * p, self.A_sq, resid=resid)
            for j in range(NFF):
                sl = []
                for half in range(2):
                    k = self.ring_slot()
                    SL = self.slots[k]
                    col0 = half * DFF + j * 128
                    self.wdma(lambda e, o=SL.v[:, 0:16, :], a=win[:, col0:col0 + 128].rearrange("(c p) n -> p c n", p=128):
                              e.dma_start(out=o, in_=a), w=[SL.all()], sem="ring%d" % k)
                    sl.append(SL)
                qs = []
                for half in range(2):
                    q = self.pair()
                    qs.append(q)
                    SL = sl[half]
                    for t in range(2):
                        for c in range(NCH):
                            self.pe(lambda e, o=ps[:, q, t, 0:TT], a=SL.v[:, c, :], b=hT.vt[:, c, t, :], c=c: e.matmul(
                                o, lhsT=a, rhs=b, start=(c == 0), stop=(c == NCH - 1)),
                                r=[SL.iv(c), hT.iv(c, t * TT, t * TT + TT)], w=[self.ps_iv(q, t)])
                sg = self.M_sg[self.sg_i]
                self.sg_i ^= 1
                qg, qu = qs
                self.act(lambda e, o=sg.vt[:, 0, :, :], a=ps[:, qg, :, 0:TT]: e.activation(out=o, in_=a, func=AF.Silu),
                         r=[self.ps_iv(qg)], w=[sg.all()])
                self.dve(lambda e, o=aT.vt[:, j, :, :], a=ps[:, qu, :, 0:TT], b=sg.vt[:, 0, :, :]: e.tensor_tensor(
                    out=o, in0=a, in1=b, op=ALU.mult),
                    r=[self.ps_iv(qu), sg.all()], w=[aT.iv(j)])
            last = (i == self.layers - 1 and s == 1 and p == 1)
            resid = self.out_proj(wout, NFF, aT, 0, p, i, kpost, True, yT, self.A_sq, pref_next=not last, defer=(p == 0))

    def proj_chunk(self, wsrc_cols, hT):
        ps = self.ps
        k = self.ring_slot()
        SL = self.slots[k]
        self.wdma(lambda e, o=SL.v[:, 0:16, :], a=wsrc_cols.rearrange("(c p) n -> p c n", p=128): e.dma_start(out=o, in_=a),
                  w=[SL.all()], sem="ring%d" % k)
        qs = [self.pair(), self.pair()]
        for t in range(NT):
            q = qs[t // 2]
            for c in range(NCH):
                self.pe(lambda e, o=ps[:, q, t % 2, 0:TT], a=SL.v[:, c, :], b=hT.vt[:, c, t, :], c=c: e.matmul(
                    o, lhsT=a, rhs=b, start=(c == 0), stop=(c == NCH - 1)),
                    r=[SL.iv(c), hT.iv(c, t * TT, t * TT + TT)], w=[self.ps_iv(qs[t // 2], t % 2)])
        return qs

    def segs(self, qs):
        ps = self.ps
        out = []
        for t in range(3):
            out.append((ps[:, qs[t // 2], t % 2, 0:TT], self.ps_iv(qs[t // 2], t % 2), "p", t * TT, TT))
        out.append((ps[:, qs[1], 1, 0:R - 900], self.ps_iv(qs[1], 1), "p", 900, R - 900))
        out.append((ps[:, qs[1], 1, R - 900:TT].rearrange("p (s k) -> p s k", k=8), self.ps_iv(qs[1], 1), "s", 0, 0))
        return out

    @staticmethod
    def eseg(E, kind, r0, n, po, sb, ss, so):
        if kind == "p":
            return E.v[:, 0, po + r0:po + r0 + n], E.iv(0, po + r0, po + r0 + n)
        ap = E.v[:, 0, sb:sb + 4 * ss].rearrange("p (s k) -> p s k", k=ss)[:, :, so:so + 8]
        return ap, E.iv(0, sb, sb + 4 * ss)

    @staticmethod
    def nseg(M, c, kind, r0, n):
        if kind == "p":
            return M.v[:, c, r0:r0 + n], M.iv(c, r0, r0 + n)
        return M.v[:, c, R:T].rearrange("p (s k) -> p s k", k=8), M.iv(c, R, T)

    def mix_even(self, i):
        dr = self.dr
        j = i // 2
        hT, mT, yT, SQ = self.B_hT, self.B_mT, self.B_yT, self.B_sq
        Ev, yc, Eu, Es, dT = self.B_Ev, self.B_yc, self.B_Eu, self.B_Es, self.B_dT
        CW, PSC, INVC, T15, PW, ps = self.M_cw, self.M_psc, self.M_invc, self.M_t15, self.M_pw, self.ps
        SO, ST, CS, IF = self.E_so, self.E_st, self.E_cs, self.M_idf
        wmi = dr["w_mix_in"][j]
        for p in range(2):
            self.prenorm(i, 2, hT, 2 * p, 2 * p, SQ)
        self.wdma(lambda e, o=PW.v.rearrange("p (g k) n -> p g k n", k=2),
                  a=dr["pool_w"][j].rearrange("g (k p) n -> p g k n", p=128): e.dma_start(out=o, in_=a),
                  w=[PW.all()], sem="pw")
        EVW = 1210
        for c in range(8):
            ea = (2, 1170, 10, 2)
            qs = self.proj_chunk(wmi[:, c * 128:(c + 1) * 128], hT)
            self.dve(lambda e, o=Ev.v[:, 0, 0:2]: e.memset(o, 0.0), w=[Ev.iv(0, 0, 2)])
            self.dma(lambda e, o=CS.v[0:8, 0, :], a=dr["sconv"][j, :, :, c * 128:(c + 1) * 128].rearrange("s r p -> (s r) p"):
                     e.dma_start(out=o, in_=a), w=[CS.all()], sem="ctxc")
            qx = self.pair()
            xiv = self.ps_iv(qx, 0, 128)
            self.pe(lambda e, o=ps[:, qx, 0, 0:8], a=CS.v[0:8, 0, :], idn=IF.v[0:8, 0, 0:8]: e.transpose(o, a, idn), r=[CS.all(), IF.all()], w=[xiv])
            self.act(lambda e, o=Ev.v[:, 0, 1170:1210].rearrange("p (s k) -> p s k", k=10)[:, :, 0:2], a=ps[:, qx, 0, 0:8].rearrange("p (s k) -> p s k", k=2):
                     e.activation(out=o, in_=a, func=AF.Copy), r=[xiv], w=[Ev.iv(0, 1170, 1210)])
            for (pap, piv, kind, r0, n) in self.segs(qs):
                o, oiv = self.eseg(Ev, kind, r0, n, *ea)
                self.act(lambda e, o=o, a=pap: e.activation(out=o, in_=a, func=AF.Copy), r=[piv], w=[oiv])
            qs = self.proj_chunk(wmi[:, (8 + c) * 128:(9 + c) * 128], hT)
            for (pap, piv, kind, r0, n) in self.segs(qs):
                o, oiv = self.eseg(Ev, kind, r0, n, *ea)
                self.dve(lambda e, o=o, a=pap: e.tensor_tensor(out=o, in0=a, in1=o, op=ALU.mult), r=[piv, oiv], w=[oiv])
            self.dve(lambda e, o=SO.v[:, 0, 0:2], a=Ev.v[:, 0, 1168:1170]: e.tensor_copy(out=o, in_=a), r=[Ev.iv(0, 1168, 1170)], w=[SO.iv(0, 0, 2)])
            self.dve(lambda e, o=SO.v[:, 0, 2:10].rearrange("p (s k) -> p s k", k=2), a=Ev.v[:, 0, 1170:1210].rearrange("p (s k) -> p s k", k=10)[:, :, 8:10]:
                     e.tensor_copy(out=o, in_=a), r=[Ev.iv(0, 1170, 1210)], w=[SO.iv(0, 2, 10)])
            qx = self.pair()
            xiv = self.ps_iv(qx, 0, 128)
            self.pe(lambda e, o=ps[0:10, qx, 0, 0:128], a=SO.v[:, 0, 0:10], idn=IF.v[:, 0, :]: e.transpose(o, a, idn), r=[SO.iv(0, 0, 10), IF.all()], w=[xiv])
            self.act(lambda e, o=ST.v[0:10, 0, :], a=ps[0:10, qx, 0, 0:128]: e.activation(out=o, in_=a, func=AF.Copy), r=[xiv], w=[ST.all()])
            self.dma(lambda e, a=ST.v[0:2, 0, :], o=dr["convp"][j, :, c * 128:(c + 1) * 128]: e.dma_start(out=o, in_=a), r=[ST.all()], sem="stc")
            self.dma(lambda e, a=ST.v[2:10, 0, :], o=dr["convs"][j, :, :, c * 128:(c + 1) * 128].rearrange("s r p -> (s r) p"):
                     e.dma_start(out=o, in_=a), r=[ST.all()], sem="stc2")
            w0, w1, w2 = [CW.v[:, 0, (j * 3 + k) * 8 + c:(j * 3 + k) * 8 + c + 1] for k in range(3)]
            NY = 1208
            self.dve(lambda e, o=yc.v[:, 0, 0:NY], a=Ev.v[:, 0, 0:NY], w0=w0: e.tensor_scalar(out=o, in0=a, scalar1=w0, scalar2=None, op0=ALU.mult),
                     r=[Ev.iv(0, 0, NY), CW.all()], w=[yc.iv(0, 0, NY)])
            for k, wk in ((1, w1), (2, w2)):
                self.dve(lambda e, o=yc.v[:, 0, 0:NY], a=Ev.v[:, 0, k:k + NY], wk=wk: e.scalar_tensor_tensor(
                    out=o, in0=a, scalar=wk, in1=o, op0=ALU.mult, op1=ALU.add),
                    r=[Ev.iv(0, k, k + NY), CW.all(), yc.iv(0, 0, NY)], w=[yc.iv(0, 0, NY)])
            qs = self.proj_chunk(wmi[:, (16 + c) * 128:(17 + c) * 128], hT)
            for (pap, piv, kind, r0, n) in self.segs(qs):
                y_ap, y_iv = self.eseg(yc, kind, r0, n, 0, 1170, 10, 0)
                o, oiv = self.nseg(mT, c, kind, r0, n)
                self.dve(lambda e, o=o, a=pap, b=y_ap: e.tensor_tensor(out=o, in0=a, in1=b, op=ALU.mult), r=[piv, y_iv], w=[oiv])
        EUW = 1275
        for c in range(8):
            g, kc = c // 2, c % 2
            wnd = 2 << g
            ea = (15, 1183, 23, 15)
            qs = self.proj_chunk(wmi[:, (24 + c) * 128:(25 + c) * 128], hT)
            self.dve(lambda e, o=Eu.v[:, 0, 0:15]: e.memset(o, 0.0), w=[Eu.iv(0, 0, 15)])
            self.dma(lambda e, o=CS.v[0:60, 0, :], a=dr["spool"][j, :, :, c * 128:(c + 1) * 128].rearrange("s r p -> (s r) p"):
                     e.dma_start(out=o, in_=a), w=[CS.all()], sem="ctxp")
            qx = self.pair()
            xiv = self.ps_iv(qx, 0, 128)
            self.pe(lambda e, o=ps[:, qx, 0, 0:60], a=CS.v[0:60, 0, :], idn=IF.v[0:60, 0, 0:60]: e.transpose(o, a, idn), r=[CS.all(), IF.all()], w=[xiv])
            self.act(lambda e, o=Eu.v[:, 0, 1183:1275].rearrange("p (s k) -> p s k", k=23)[:, :, 0:15], a=ps[:, qx, 0, 0:60].rearrange("p (s k) -> p s k", k=15):
                     e.activation(out=o, in_=a, func=AF.Copy), r=[xiv], w=[Eu.iv(0, 1183, 1275)])
            for (pap, piv, kind, r0, n) in self.segs(qs):
                o, oiv = self.eseg(Eu, kind, r0, n, *ea)
                self.act(lambda e, o=o, a=pap: e.activation(out=o, in_=a, func=AF.Copy), r=[piv], w=[oiv])
            self.dve(lambda e, o=SO.v[:, 0, 0:15], a=Eu.v[:, 0, 1168:1183]: e.tensor_copy(out=o, in_=a), r=[Eu.iv(0, 1168, 1183)], w=[SO.iv(0, 0, 15)])
            self.dve(lambda e, o=SO.v[:, 0, 15:75].rearrange("p (s k) -> p s k", k=15), a=Eu.v[:, 0, 1183:1275].rearrange("p (s k) -> p s k", k=23)[:, :, 8:23]:
                     e.tensor_copy(out=o, in_=a), r=[Eu.iv(0, 1183, 1275)], w=[SO.iv(0, 15, 75)])
            qx = self.pair()
            xiv = self.ps_iv(qx, 0, 128)
            self.pe(lambda e, o=ps[0:75, qx, 0, 0:128], a=SO.v[:, 0, 0:75], idn=IF.v[:, 0, :]: e.transpose(o, a, idn), r=[SO.iv(0, 0, 75), IF.all()], w=[xiv])
            self.act(lambda e, o=ST.v[0:75, 0, :], a=ps[0:75, qx, 0, 0:128]: e.activation(out=o, in_=a, func=AF.Copy), r=[xiv], w=[ST.all()])
            self.dma(lambda e, a=ST.v[0:15, 0, :], o=dr["poolp"][j, :, c * 128:(c + 1) * 128]: e.dma_start(out=o, in_=a), r=[ST.all()], sem="stp")
            self.dma(lambda e, a=ST.v[15:75, 0, :], o=dr["pools"][j, :, :, c * 128:(c + 1) * 128].rearrange("s r p -> (s r) p"):
                     e.dma_start(out=o, in_=a), r=[ST.all()], sem="stp2")
            lo = wnd - 1
            n = EUW - lo
            self.dve(lambda e, o=Es.v[:, 0, lo:EUW], a=Eu.v[:, 0, lo:EUW], b=Eu.v[:, 0, lo - 1:EUW - 1]: e.tensor_tensor(out=o, in0=a, in1=b, op=ALU.add),
                     r=[Eu.iv(0, 0, EUW)], w=[Es.iv(0, lo, EUW)])
            for sh in range(2, wnd):
                self.dve(lambda e, o=Es.v[:, 0, lo:EUW], b=Eu.v[:, 0, lo - sh:EUW - sh]: e.tensor_tensor(out=o, in0=o, in1=b, op=ALU.add),
                         r=[Eu.iv(0, 0, EUW), Es.iv(0, lo, EUW)], w=[Es.iv(0, lo, EUW)])
            for (kind, r0, n_) in (("p", 0, R), ("s", 0, 0)):
                s_ap, s_iv = self.eseg(Es, kind, r0, n_, *ea)
                u_ap, u_iv = self.eseg(Eu, kind, r0, n_, *ea)
                o, oiv = self.nseg(dT, kc, kind, r0, n_)
                self.dve(lambda e, o=o, a=s_ap, b=u_ap, wnd=wnd: e.scalar_tensor_tensor(out=o, in0=a, scalar=1.0 / wnd, in1=b, op0=ALU.mult, op1=ALU.subtract),
                         r=[s_iv, u_iv], w=[oiv])
            self.dve(lambda e, o=T15.v[:, 0, 0:15], a=Es.v[:, 0, 15:30], b=INVC.v[:, 0, g * 15:(g + 1) * 15]: e.tensor_tensor(out=o, in0=a, in1=b, op=ALU.mult),
                     r=[Es.iv(0, 15, 30), INVC.all()], w=[T15.all()])
            self.dve(lambda e, o=dT.v[:, kc, 0:15], a=T15.v[:, 0, 0:15], b=Eu.v[:, 0, 15:30]: e.tensor_tensor(out=o, in0=a, in1=b, op=ALU.subtract),
                     r=[T15.all(), Eu.iv(0, 15, 30)], w=[dT.iv(kc, 0, 15)])
            if kc == 1:
                for m in range(2):
                    qs = [self.pair(), self.pair()]
                    for t in range(NT):
                        for k2 in range(2):
                            self.pe(lambda e, o=ps[:, qs[t // 2], t % 2, 0:TT], a=PW.v[:, g * 2 + k2, m * 128:(m + 1) * 128], b=dT.vt[:, k2, t, :], k2=k2: e.matmul(
                                o, lhsT=a, rhs=b, start=(k2 == 0), stop=(k2 == 1)),
                                r=[PW.all(), dT.iv(k2, t * TT, t * TT + TT)], w=[self.ps_iv(qs[t // 2], t % 2)])
                    ch = 8 + 2 * g + m
                    sc = PSC.v[:, 0, j * 8 + 2 * g + m:j * 8 + 2 * g + m + 1]
                    for hq in range(2):
                        self.act(lambda e, o=mT.vt[:, ch, 2 * hq:2 * hq + 2, :], a=ps[:, qs[hq], :, 0:TT], sc=sc: e.activation(out=o, in_=a, func=AF.Copy, scale=sc),
                                 r=[self.ps_iv(qs[hq]), PSC.all()], w=[mT.iv(ch, 2 * hq * TT, 2 * hq * TT + TP)])
        for p in range(2):
            self.out_proj(dr["w_mix_out"][j], NCH, mT, 2 * p, p, i, 3, False, yT, SQ, pref_next=(p == 1))


    def pool_op(self, fn, r=(), w=()):
        return self.S.add("pool", fn, r, w)

    def attn_front(self, g):
        ps = self.ps
        qT, SK = self.C_qT, self.M_sink
        if g.get("pre") is not None:
            g["pre"]()
        h0, j, qcols, nq, c0, c1, kparts, B = g["h0"], g["j"], g["qcols"], g["nq"], g["c0"], g["c1"], g["kparts"], g["B"]
        S, P_, PT, SM = self.at_sets[self.at_i]
        P = self.at_P3[self.at_p3]
        self.at_p3 = (self.at_p3 + 1) % 3
        g["Dg"] = self.at_diag[self.at_i]
        self.at_i ^= 1
        g["set"] = (S, P, PT, SM)
        n = c1 - c0
        q1 = self.pair()
        Sps = ps[:, q1].rearrange("p t (h n) -> p (t h) n", n=256)
        s_iv = self.ps_iv(q1)
        HS = (0, 2, 1, 3)
        for sl_ in range(4):
            h = h0 + HS[sl_]
            base = 64 * (h % 2)
            ch = h // 2
            for (reg, kvi, lo, nn, soff) in kparts:
                self.pe(lambda e, o=Sps[0:nq, sl_, soff:soff + nn], a=qT.v[base:base + 64, ch, qcols:qcols + nq],
                        b=reg.v[base:base + 64, kvi, lo:lo + nn]: e.matmul(o, lhsT=a, rhs=b, start=True, stop=True),
                        r=[qT.iv(ch, qcols, qcols + nq), reg.iv(kvi, lo, lo + nn)], w=[s_iv])
        Sv = S.v.rearrange("p o (h n) -> p (o h) n", n=256)
        Bv = B.v.rearrange("p o (h n) -> p (o h) n", n=256)
        Sw, Bw = Sv[0:nq, :, c0:c1], Bv[0:nq, :, c0:c1]
        mx, den, esk, rden = [SM.v[0:nq, 0, 4 * k:4 * k + 4] for k in range(4)]
        negm = rden
        sk = SK.v[0:nq, 0, j * 32 + h0:j * 32 + h0 + 4]
        X_ = mybir.AxisListType.X
        self.dve(lambda e, o=Sw, a=Sps[0:nq, :, c0:c1], b=Bw: e.scalar_tensor_tensor(out=o, in0=a, scalar=0.125, in1=b, op0=ALU.mult, op1=ALU.add),
                 r=[s_iv, B.all()], w=[S.all()])
        self.dve(lambda e, o=mx, a=Sw: e.tensor_reduce(out=o, in_=a, axis=X_, op=ALU.max), r=[S.all()], w=[SM.iv(0, 0, 4)])
        self.dve(lambda e, o=mx, b=sk: e.tensor_tensor(out=o, in0=o, in1=b, op=ALU.max), r=[SM.iv(0, 0, 4), SK.all()], w=[SM.iv(0, 0, 4)])
        self.dve(lambda e, o=negm, a=mx: e.tensor_scalar(out=o, in0=a, scalar1=-1.0, scalar2=None, op0=ALU.mult), r=[SM.iv(0, 0, 4)], w=[SM.iv(0, 12, 16)])
        self.dve(lambda e, o=esk, a=sk, b=mx: e.tensor_tensor(out=o, in0=a, in1=b, op=ALU.subtract), r=[SM.iv(0, 0, 4), SK.all()], w=[SM.iv(0, 8, 12)])
        Pv = P.v.rearrange("p o (h n) -> p (o h) n", n=256)
        for sl_ in range(4):
            self.act(lambda e, o=Pv[0:nq, sl_, c0:c1], a=Sv[0:nq, sl_, c0:c1], nb=SM.v[0:nq, 0, 12 + sl_:13 + sl_], d=SM.v[0:nq, 0, 4 + sl_:5 + sl_]: e.activation(
                out=o, in_=a, func=AF.Exp, bias=nb, scale=1.0, accum_out=d),
                r=[S.iv(0, sl_ * 256, sl_ * 256 + 256), SM.iv(0, 12 + sl_, 13 + sl_)], w=[P.iv(0, sl_ * 256, sl_ * 256 + 256), SM.iv(0, 4 + sl_, 5 + sl_)])
        self.act(lambda e, o=esk: e.activation(out=o, in_=o, func=AF.Exp), r=[SM.iv(0, 8, 12)], w=[SM.iv(0, 8, 12)])
        return g

    def attn_back1(self, g):
        IB = self.M_idb
        nq = g["nq"]
        S, P, PT, SM = g["set"]
        Dg = g["Dg"]
        mx, den, esk, rden = [SM.v[0:nq, 0, 4 * k:4 * k + 4] for k in range(4)]
        self.dve(lambda e, o=den, b=esk: e.tensor_tensor(out=o, in0=o, in1=b, op=ALU.add), r=[SM.iv(0, 4, 12)], w=[SM.iv(0, 4, 8)])
        self.dve(lambda e, o=rden, a=den: e.reciprocal(out=o, in_=a), r=[SM.iv(0, 4, 8)], w=[SM.iv(0, 12, 16)])
        Dv = Dg.v.rearrange("p o (h n) -> p (o h) n", n=128)
        self.pool_op(lambda e, o=Dv[0:nq, :, 0:nq], a=IB.v[0:nq, 0, 0:nq].unsqueeze(1).broadcast_to([nq, 4, nq]),
                     b=rden.unsqueeze(2).broadcast_to([nq, 4, nq]): e.tensor_tensor(out=o, in0=a, in1=b, op=ALU.mult),
                     r=[IB.all(), SM.iv(0, 12, 16)], w=[Dg.all()])

    def attn_back1c(self, g):
        ps = self.ps
        nq, vparts = g["nq"], g["vparts"]
        S, P, PT, SM = g["set"]
        Dg = g["Dg"]
        Pv = P.v.rearrange("p o (h n) -> p (o h) n", n=256)
        Dv = Dg.v.rearrange("p o (h n) -> p (o h) n", n=128)
        q2 = self.pair()
        PTps = ps[:, q2].rearrange("p t (b n) -> p (t b) n", n=128)
        pt_iv = self.ps_iv(q2)
        for hh in range(4):
            for (vfn, viv, nn, soff, bi) in vparts:
                self.pe(lambda e, o=PTps[0:nn, bi * 4 + hh, 0:nq], a=Pv[0:nq, hh, soff:soff + nn], b=Dv[0:nq, hh, 0:nq]: e.matmul(
                    o, lhsT=a, rhs=b, start=True, stop=True), r=[P.all(), Dg.all()], w=[pt_iv])
        PTv = PT.v.rearrange("p o (b n) -> p (o b) n", n=128)
        self.act(lambda e, o=PTv[:, :, 0:nq], a=PTps[:, :, 0:nq]: e.activation(out=o, in_=a, func=AF.Copy), r=[pt_iv], w=[PT.all()])

    def attn_back2(self, g):
        ps, attT = self.ps, self.C_attT
        h0, qcols, nq, vparts = g["h0"], g["qcols"], g["nq"], g["vparts"]
        S, P, PT, SM = g["set"]
        q3 = self.pair()
        kv = h0 // 8
        ch0 = h0 // 2
        PTv = PT.v.rearrange("p o (b n) -> p (o b) n", n=128)
        Ops = ps[:, q3, 0, :].rearrange("p (h n) -> p h n", n=128)
        o_iv = ("ps", q3 * 4096, q3 * 4096 + 2048)
        nb = len(vparts)
        for k_, (vfn, viv, nn, soff, bi) in enumerate(vparts):
            self.pe(lambda e, o=Ops[0:64, :, 0:nq], a=vfn(kv), b=PTv[0:nn, bi * 4:bi * 4 + 4, 0:nq], k_=k_: e.matmul(
                o, lhsT=a, rhs=b, start=(k_ == 0), stop=(k_ == nb - 1)), r=[viv, PT.all()], w=[o_iv])
        self.act(lambda e, o=attT.v[0:64, ch0:ch0 + 2, qcols:qcols + nq], a=Ops[0:64, 0:2, 0:nq]: e.activation(out=o, in_=a, func=AF.Copy),
                 r=[o_iv], w=attT.ivs(ch0, ch0 + 2, qcols, qcols + nq))
        self.act(lambda e, o=attT.v[64:128, ch0:ch0 + 2, qcols:qcols + nq], a=Ops[0:64, 2:4, 0:nq]: e.activation(out=o, in_=a, func=AF.Copy),
                 r=[o_iv], w=attT.ivs(ch0, ch0 + 2, qcols, qcols + nq))

    def attn_pipeline(self, groups):
        n = len(groups)
        for k in range(n + 5):
            if 0 <= k - 5 < n:
                self.attn_back2(groups[k - 5])
            if 0 <= k - 3 < n:
                self.attn_back1c(groups[k - 3])
            if k < n:
                self.attn_front(groups[k])
            if 0 <= k - 1 < n:
                self.attn_back1(groups[k - 1])

    def mix_odd(self, i):
        dr = self.dr
        j = i // 2
        ps = self.ps
        hT, qT, attT, yT, kdT, V, SQ = self.C_hT, self.C_qT, self.C_attT, self.C_yT, self.C_kdT, self.C_V, self.B_sq
        KVf, KcT, Vc, STG, IB, IF = self.C_KVf, self.C_KcT, self.C_Vc, self.M_stg, self.M_idb, self.M_idf
        wq = dr["w_qkv"][j]
        for p in range(2):
            self.prenorm(i, 2, hT, 2 * p, 2 * p, SQ)
        for cp in range(2):
            qs = self.proj_chunk(wq[:, 2048 + cp * 128:2048 + (cp + 1) * 128], hT)
            for par in range(2):
                kv = 2 * cp + par
                for hq in range(2):
                    for dst in range(2):
                        self.act(lambda e, o=kdT.vt[dst * 64:dst * 64 + 64, kv, 2 * hq:2 * hq + 2, :], a=ps[par * 64:par * 64 + 64, qs[hq], :, 0:TT]:
                                 e.activation(out=o, in_=a, func=AF.Copy),
                                 r=[self.ps_iv(qs[hq])], w=[kdT.iv(kv, 2 * hq * TT, 2 * hq * TT + TP)])
            self.dve(lambda e, o=KVf.v[:, cp, :], a=ps[:, qs[1], 1, 140:300]: e.tensor_copy(out=o, in_=a),
                     r=[self.ps_iv(qs[1], 1)], w=[KVf.iv(cp)])
        for cp in range(2):
            qs = self.proj_chunk(wq[:, 2304 + cp * 128:2304 + (cp + 1) * 128], hT)
            for hq in range(2):
                self.act(lambda e, o=qT.vt[:, 14 + cp, 2 * hq:2 * hq + 2, :], a=ps[:, qs[hq], :, 0:TT]: e.activation(out=o, in_=a, func=AF.Copy),
                         r=[self.ps_iv(qs[hq])], w=[qT.iv(14 + cp, 2 * hq * TT, 2 * hq * TT + TP)])
            self.dve(lambda e, o=KVf.v[:, 2 + cp, :], a=ps[:, qs[1], 1, 140:300]: e.tensor_copy(out=o, in_=a),
                     r=[self.ps_iv(qs[1], 1)], w=[KVf.iv(2 + cp)])
        blocks = [(b * 128, 128) for b in range(9)] + [(1152, 16), (1168, 32)]
        for b, (k0, n) in enumerate(blocks):
            q = self.pair()
            psb = ps[:, q, 0, 0:128].bitcast(BF16)
            piv = ("ps", q * 4096, q * 4096 + 512)
            for cp in range(2):
                self.pe(lambda e, o=psb[0:n, cp * 128:(cp + 1) * 128], a=qT.v[:, 14 + cp, k0:k0 + n], idn=IB.v[:, 0, :]: e.transpose(o, a, idn),
                        r=[qT.iv(14 + cp, k0, k0 + n), IB.all()], w=[piv])
            self.act(lambda e, o=V.v[0:n, b, :], a=psb[0:n, 0:256]: e.activation(out=o, in_=a, func=AF.Copy), r=[piv], w=[V.iv(b)])
        for (c0, n, which) in ((0, 128, "p"), (128, 32, "s")):
            q = self.pair()
            piv = self.ps_iv(q, 0, 512)
            for ch in range(4):
                self.pe(lambda e, o=ps[0:n, q, 0, ch * 128:(ch + 1) * 128], a=KVf.v[:, ch, c0:c0 + n], idn=IF.v[:, 0, :]: e.transpose(o, a, idn),
                        r=[KVf.iv(ch, c0, c0 + n), IF.all()], w=[piv])
            self.act(lambda e, o=STG.v[0:n, 0, :], a=ps[0:n, q, 0, 0:512]: e.activation(out=o, in_=a, func=AF.Copy), r=[piv], w=[STG.all()])
            if which == "p":
                self.dma(lambda e, o=dr["kp"][j], a=STG.v[:, 0, 0:256]: e.dma_start(out=o, in_=a), r=[STG.all()], sem="okp")
                self.dma(lambda e, o=dr["vp"][j], a=STG.v[:, 0, 256:512]: e.dma_start(out=o, in_=a), r=[STG.all()], sem="ovp")
            else:
                for sq in range(4):
                    self.dma(lambda e, o=dr["ks"][j, sq, 120:128, :], a=STG.v[8 * sq:8 * sq + 8, 0, 0:256]: e.dma_start(out=o, in_=a), r=[STG.all()], sem="oks%d" % sq)
                    self.dma(lambda e, o=dr["vs"][j, sq, 120:128, :], a=STG.v[8 * sq:8 * sq + 8, 0, 256:512]: e.dma_start(out=o, in_=a), r=[STG.all()], sem="ovs%d" % sq)
                    self.dma(lambda e, o=dr["ks"][j, sq, 0:120, :], a=dr["ck"][j, sq, 8:128, :]: e.dma_start(out=o, in_=a), sem="cks%d" % sq)
                    self.dma(lambda e, o=dr["vs"][j, sq, 0:120, :], a=dr["cv"][j, sq, 8:128, :]: e.dma_start(out=o, in_=a), sem="cvs%d" % sq)
        for c in range(NCH):
            qs = self.proj_chunk(wq[:, c * 128:(c + 1) * 128], hT)
            for hq in range(2):
                self.act(lambda e, o=qT.vt[:, c, 2 * hq:2 * hq + 2, :], a=ps[:, qs[hq], :, 0:TT]: e.activation(out=o, in_=a, func=AF.Copy),
                         r=[self.ps_iv(qs[hq])], w=[qT.iv(c, 2 * hq * TT, 2 * hq * TT + TP)])
        groups = []
        for hg in range(8):
            h0 = 4 * hg
            B = self.at_bias[hg % 2]

            def pre_p(B=B, h0=h0, hg=hg):
                self.dma(lambda e, o=B.v.rearrange("p o (h n) -> p (o h) n", n=256), a=dr["biasP"][h0:h0 + 4].rearrange("h q s -> q h s"):
                         e.dma_start(out=o, in_=a), w=[B.all()], sem="bias%d" % (hg % 2))
            for qb in range(10):
                q0 = qb * 128
                nq = 128 if qb < 9 else 16
                if qb == 0:
                    kparts = [(kdT, h0 // 8, 0, nq, 128)]
                    vparts = [(lambda kv: V.v[0:128, 0, kv * 64:(kv + 1) * 64], V.iv(0), nq, 128, 1)]
                    c0, c1 = 128, 128 + nq
                else:
                    kparts = [(kdT, h0 // 8, q0 - 128, 128 + nq, 0)]
                    vparts = [(lambda kv, qb=qb: V.v[0:128, qb - 1, kv * 64:(kv + 1) * 64], V.iv(qb - 1), 128, 0, 0),
                              (lambda kv, qb=qb, nq=nq: V.v[0:nq, qb, kv * 64:(kv + 1) * 64], V.iv(qb), nq, 128, 1)]
                    c0, c1 = 0, 128 + nq
                groups.append(dict(h0=h0, j=j, qcols=q0, nq=nq, c0=c0, c1=c1, kparts=kparts, vparts=vparts, B=B,
                                   pre=pre_p if qb == 0 else None))
        for sq in range(4):
            def pre_seq(sq=sq):
                self.dma(lambda e, o=STG.v[:, 0, 0:256], a=dr["ck"][j, sq]: e.dma_start(out=o, in_=a), w=[STG.iv(0, 0, 256)], sem="ckl")
                self.wdma(lambda e, o=Vc.v[:, 0, :], a=dr["cv"][j, sq]: e.dma_start(out=o, in_=a), w=[Vc.all()], sem="cvl")
                q = self.pair()
                piv = self.ps_iv(q, 0, 512)
                for kv in range(4):
                    self.pe(lambda e, o=ps[0:64, q, 0, kv * 128:(kv + 1) * 128], a=STG.v[:, 0, kv * 64:(kv + 1) * 64], idn=IF.v[:, 0, :]: e.transpose(o, a, idn),
                            r=[STG.iv(0, 0, 256), IF.all()], w=[piv])
                for dst in range(2):
                    self.act(lambda e, o=KcT.v[dst * 64:dst * 64 + 64, :, :], a=ps[0:64, q, 0, 0:512].rearrange("p (k n) -> p k n", k=4):
                             e.activation(out=o, in_=a, func=AF.Copy), r=[piv], w=[KcT.all()])
            for hg in range(8):
                h0 = 4 * hg
                B = self.at_bias[hg % 2]

                def pre_s(B=B, h0=h0, hg=hg, sq=sq, first=(hg == 0)):
                    if first:
                        pre_seq(sq)
                    self.dma(lambda e, o=B.v.rearrange("p o (h n) -> p (o h) n", n=256)[0:8, :, 0:160], a=dr["biasS"][h0:h0 + 4, sq].rearrange("h q s -> q h s"):
                             e.dma_start(out=o, in_=a), w=[B.all()], sem="bias%d" % (hg % 2))
                q0 = R + 8 * sq
                kparts = [(KcT, h0 // 8, 0, 128, 0), (kdT, h0 // 8, R, 32, 128)]
                vparts = [(lambda kv: Vc.v[:, 0, kv * 64:(kv + 1) * 64], Vc.all(), 128, 0, 0),
                          (lambda kv: V.v[0:32, 10, kv * 64:(kv + 1) * 64], V.iv(10), 32, 128, 1)]
                groups.append(dict(h0=h0, j=j, qcols=q0, nq=8, c0=0, c1=160, kparts=kparts, vparts=vparts, B=B, pre=pre_s))
            self.attn_pipeline(groups)
            groups = []
        for p in range(2):
            self.out_proj(dr["w_o"][j], NCH, attT, 2 * p, p, i, 3, False, yT, SQ, pref_next=(p == 1))


_NC_CACHE = {}


def get_nc(layers=DEPTH, phases=("ffn0", "mix", "ffn1")):
    key = (layers, tuple(phases))
    if key not in _NC_CACHE:
        _NC_CACHE[key] = Builder(layers, phases).build()
    return _NC_CACHE[key]


def _t5_bucket(dist):
    n = np.maximum(dist, 0)
    ratio = np.log(np.maximum(n, 1).astype(np.float32) / np.float32(16)) / np.float32(math.log(128 / 16))
    large = np.minimum(16 + (ratio * np.float32(16)).astype(np.int32), 31)
    return np.where(n < 16, n, large)


def make_in_maps(inp, ncores=8):
    f = lambda k: np.asarray(inp[k], np.float32)
    xp, xs, g = f("x_prompt"), f("x_sample"), f("norm_g")
    gT = np.ascontiguousarray(g.reshape(DEPTH * 6, NCH, 128).transpose(2, 0, 1).reshape(128, -1))
    cw = f("conv_w")
    cwT = np.ascontiguousarray(cw.reshape(2, 3, 8, 128).transpose(3, 0, 1, 2).reshape(128, 48))
    psc = f("pool_scale")
    pscT = np.ascontiguousarray(psc.reshape(2, 8, 128).transpose(2, 0, 1).reshape(128, 16))
    sinkB = np.ascontiguousarray(np.broadcast_to(f("attn_sinks").reshape(1, 64), (128, 64)))
    ident = np.eye(128, dtype=np.float32)
    rb = f("rel_bias")
    qi = np.arange(128)[:, None]
    sj = np.arange(256)[None, :]
    dist = 128 + qi - sj
    valid = (dist >= 0) & (dist <= 128)
    bk = _t5_bucket(dist)
    biasP = np.where(valid[None], rb[bk].transpose(2, 0, 1), np.float32(NEG)).astype(np.float32)
    biasS = np.full((32, 4, 8, 160), NEG, np.float32)
    q8 = np.arange(8)[:, None]
    sc = np.arange(128)[None, :]
    dc = q8 + 128 - sc
    vc = (dc >= 0) & (dc <= 128)
    bc = np.where(vc[None], rb[_t5_bucket(dc)].transpose(2, 0, 1), np.float32(NEG))
    i8 = np.arange(8)[None, :]
    dn = q8 - i8
    vn = dn >= 0
    bn = np.where(vn[None], rb[_t5_bucket(dn)].transpose(2, 0, 1), np.float32(NEG))
    for s_ in range(4):
        biasS[:, s_, :, 0:128] = bc
        biasS[:, s_, :, 128 + 8 * s_:136 + 8 * s_] = bn
    perm = np.array([g4 * 4 + o_ for g4 in range(8) for o_ in (0, 2, 1, 3)])
    biasP = biasP[perm]
    biasS = np.ascontiguousarray(biasS[perm])
    sinkB = np.ascontiguousarray(sinkB.reshape(128, 2, 32)[:, :, perm].reshape(128, 64))
    shared = {"gT": gT, "cwT": cwT, "pscT": pscT, "sinkB": sinkB, "ident": ident, "biasP": np.ascontiguousarray(biasP),
              "biasS": biasS}
    for k_ in ("w_ffn_in", "w_ffn_out", "w_mix_in", "w_mix_out", "pool_w", "w_qkv", "w_o"):
        shared[k_] = f(k_)
    sconv, spool = f("state_conv"), f("state_pool")
    ck = f("cache_k").reshape(2, 32, 128, 256)
    cv = f("cache_v").reshape(2, 32, 128, 256)
    maps = []
    for core in range(ncores):
        b, half = core // 2, core % 2
        r0 = 0 if half == 0 else HALO0
        rows = np.concatenate([xp[b, r0:r0 + R], xs[4 * core:4 * core + 4].reshape(NS, D)], axis=0)
        xT = np.ascontiguousarray(rows.reshape(T, NCH, 128).transpose(2, 1, 0))
        invc = np.zeros((128, 4, 15), np.float32)
        for gi, wn in enumerate((2, 4, 8, 16)):
            cnt = np.minimum(wn, r0 + np.arange(15) + 1).astype(np.float32)
            invc[:, gi, :] = (np.float32(1.0) / cnt)[None, :]
        m = dict(shared)
        m.update({"xT": xT, "invc": np.ascontiguousarray(invc.reshape(128, 60)),
                  "sconv": np.ascontiguousarray(sconv[:, 4 * core:4 * core + 4]),
                  "spool": np.ascontiguousarray(spool[:, 4 * core:4 * core + 4]),
                  "ck": np.ascontiguousarray(ck[:, 4 * core:4 * core + 4]),
                  "cv": np.ascontiguousarray(cv[:, 4 * core:4 * core + 4])})
        maps.append(m)
    return maps


def assemble(results):
    yp = np.zeros((4, 2048, D), np.float32)
    ys = np.zeros((32, 8, D), np.float32)
    convp = np.zeros((2, 4, 2, 1024), np.float32)
    poolp = np.zeros((2, 4, 15, 1024), np.float32)
    kp = np.zeros((2, 4, 128, 4, 64), np.float32)
    vp = np.zeros((2, 4, 128, 4, 64), np.float32)
    convs = np.zeros((2, 32, 2, 1024), np.float32)
    pools = np.zeros((2, 32, 15, 1024), np.float32)
    ks = np.zeros((2, 32, 128, 4, 64), np.float32)
    vs = np.zeros((2, 32, 128, 4, 64), np.float32)
    for core, res in enumerate(results):
        rows = np.asarray(res["yT"]).transpose(2, 1, 0).reshape(T, D)
        b, half = core // 2, core % 2
        if half == 0:
            yp[b, 0:R] = rows[0:R]
        else:
            yp[b, R:2048] = rows[2 * R - 2048:R]
            convp[:, b] = np.asarray(res["convp"])
            poolp[:, b] = np.asarray(res["poolp"])
            kp[:, b] = np.asarray(res["kp"]).reshape(2, 128, 4, 64)
            vp[:, b] = np.asarray(res["vp"]).reshape(2, 128, 4, 64)
        ys[4 * core:4 * core + 4] = rows[R:].reshape(4, 8, D)
        convs[:, 4 * core:4 * core + 4] = np.asarray(res["convs"])
        pools[:, 4 * core:4 * core + 4] = np.asarray(res["pools"])
        ks[:, 4 * core:4 * core + 4] = np.asarray(res["ks"]).reshape(2, 4, 128, 4, 64)
        vs[:, 4 * core:4 * core + 4] = np.asarray(res["vs"]).reshape(2, 4, 128, 4, 64)
    return (yp, ys, convp, poolp, kp, vp, convs, pools, ks, vs)


def kernel(**inputs):
    nc = get_nc()
    maps = make_in_maps(inputs)
    res = run_bass_kernel_spmd(nc, maps, core_ids=list(range(8)))
    return assemble(res.results)
```

```python
import math
from contextlib import ExitStack

import numpy as np
import concourse.bass as bass
import concourse.mybir as mybir
from concourse.bass_utils import run_bass_kernel_spmd

F32 = mybir.dt.float32
BF16 = mybir.dt.bfloat16
AF = mybir.ActivationFunctionType
ALU = mybir.AluOpType

D = 2048
NCH = 16
DFF = 5504
NFF = 43
DEPTH = 4
R = 1168
NS = 32
T = R + NS
TT = 300
NT = T // TT
TP = 600
HALO0 = 2048 - R
EPS = 1e-6
NEG = -1e30

ENGS = ("pe", "act", "dve", "pool", "sp")


class Op:
    __slots__ = ("eng", "fn", "deps", "sig", "dma_sem", "dma_val", "idx")

    def __init__(self, eng, fn):
        self.eng = eng
        self.fn = fn
        self.deps = []
        self.sig = False
        self.dma_sem = None
        self.dma_val = 0
        self.idx = 0


class Sched:
    def __init__(self):
        self.ops = {e: [] for e in ENGS}
        self.res = {}
        self.dma_cnt = {}
        self.dma_last = {}
        self.all_dma = []

    def _overlaps(self, space, lo, hi):
        lst = self.res.setdefault(space, [])
        return [r for r in lst if r[0] < hi and lo < r[1]]

    def add(self, eng, fn, r=(), w=(), dma=None):
        op = Op(eng, fn)
        deps = {}

        def dep(d, raw):
            if d is None or d is op:
                return
            if d.dma_sem is None and d.eng == eng and not raw:
                return
            deps[id(d)] = d

        rkey = eng if dma is None else ("dma", id(op))
        for (space, lo, hi) in r:
            for rec in self._overlaps(space, lo, hi):
                dep(rec[2], True)
                rec[3][rkey] = op
        for (space, lo, hi) in w:
            lst = self.res.setdefault(space, [])
            new = []
            for rec in lst:
                if rec[0] < hi and lo < rec[1]:
                    dep(rec[2], False)
                    for rd in rec[3].values():
                        dep(rd, False)
                    if rec[0] < lo:
                        new.append([rec[0], lo, rec[2], dict(rec[3])])
                    if hi < rec[1]:
                        new.append([hi, rec[1], rec[2], dict(rec[3])])
                else:
                    new.append(rec)
            new.append([lo, hi, op, {}])
            self.res[space] = new
        if dma is not None:
            prev = self.dma_last.get(dma)
            if prev is not None:
                deps[id(prev)] = prev
            op.dma_sem = dma
            self.dma_cnt[dma] = self.dma_cnt.get(dma, 0) + 16
            op.dma_val = self.dma_cnt[dma]
            self.dma_last[dma] = op
            self.all_dma.append(op)
        op.deps = list(deps.values())
        for d in op.deps:
            d.sig = True
        self.ops[eng].append(op)
        return op

    def emit(self, nc, stack):
        engs = {"pe": nc.tensor, "act": nc.scalar, "dve": nc.vector, "pool": nc.gpsimd, "sp": nc.sync}
        esem = {e: stack.enter_context(nc.semaphore("es_" + e)) for e in ENGS}
        dsem = {n: stack.enter_context(nc.semaphore("ds_" + n)) for n in self.dma_cnt}
        for e in ENGS:
            k = 0
            for op in self.ops[e]:
                if op.dma_sem is None and op.sig:
                    k += 1
                    op.idx = k
        final = [(dsem[n], v) for n, v in self.dma_cnt.items()]
        block = stack.enter_context(nc.Block())
        ops = self.ops

        def make(e):
            def body(eng):
                waited = {}
                for op in ops[e]:
                    for d in op.deps:
                        if d.dma_sem is not None:
                            key, sem, val = "d_" + d.dma_sem, dsem[d.dma_sem], d.dma_val
                        else:
                            key, sem, val = "e_" + d.eng, esem[d.eng], d.idx
                        if waited.get(key, 0) >= val:
                            continue
                        waited[key] = val
                        eng.wait_ge(sem, val)
                    ins = op.fn(eng)
                    if op.dma_sem is not None:
                        ins.then_inc(dsem[op.dma_sem], 16)
                    elif op.sig:
                        ins.then_inc(esem[e], 1)
                if e == "sp":
                    for sem, val in final:
                        eng.wait_ge(sem, val)
            return body

        block.tensor(make("pe"))
        block.scalar(make("act"))
        block.vector(make("dve"))
        block.gpsimd(make("pool"))
        block.sync(make("sp"))


class Reg:
    def __init__(self, space, base_t, off, dt, nch, ncol, tile=None):
        self.space = space
        self.off = off
        self.es = 2 if dt == BF16 else 4
        self.nch = nch
        self.ncol = ncol
        nbytes = nch * ncol * self.es
        assert off % 4 == 0 and nbytes % 4 == 0
        self.nbytes = nbytes
        v = base_t[:, off // 4:(off + nbytes) // 4]
        if dt != F32:
            v = v.bitcast(dt)
        self.v = v.rearrange("p (c n) -> p c n", c=nch)
        self.vt = None
        if tile is not None:
            self.tile = tile
            self.vt = v.rearrange("p (c t n) -> p c t n", c=nch, n=tile)

    def iv(self, c, lo=0, hi=None):
        hi = self.ncol if hi is None else hi
        return (self.space, self.off + (c * self.ncol + lo) * self.es, self.off + (c * self.ncol + hi) * self.es)

    def ivs(self, c0, c1, lo=0, hi=None):
        return [self.iv(c, lo, hi) for c in range(c0, c1)]

    def all(self):
        return (self.space, self.off, self.off + self.nbytes)


class Builder:
    def __init__(self, layers=DEPTH, phases=("ffn0", "mix", "ffn1")):
        self.layers = layers
        self.phases = phases
        self.S = Sched()
        self.bank = 0
        self.reserved = set()
        self.pref = {}
        self.pending = None

    def pe(self, fn, r=(), w=()):
        return self.S.add("pe", fn, r, w)

    def act(self, fn, r=(), w=()):
        return self.S.add("act", fn, r, w)

    def dve(self, fn, r=(), w=()):
        return self.S.add("dve", fn, r, w)

    def wdma(self, fn, r=(), w=(), sem=None):
        return self.S.add("pool", fn, r, w, dma=sem)

    def dma(self, fn, r=(), w=(), sem=None):
        return self.S.add("sp", fn, r, w, dma=sem)

    def pair(self):
        while True:
            q = self.bank
            self.bank = (self.bank + 1) % 4
            if q not in self.reserved:
                return q

    def ps_iv(self, q, t=None, n=TT):
        base = q * 4096
        if t is None:
            return ("ps", base, base + 4096)
        return ("ps", base + t * 2048, base + t * 2048 + n * 4)

    def ring_slot(self):
        s = self.ring_i
        self.ring_i = (self.ring_i + 1) % self.NSLOT
        return s

    def build(self):
        nc = bass.Bass("TRN2", target_bir_lowering=False)
        self.nc = nc
        L = self.layers
        dr = {}

        def din(name, shape):
            dr[name] = nc.dram_tensor(name, list(shape), F32, kind="ExternalInput").ap()

        def dout(name, shape):
            dr[name] = nc.dram_tensor(name, list(shape), F32, kind="ExternalOutput").ap()

        din("xT", (128, NCH, T))
        din("gT", (128, DEPTH * 6 * NCH))
        din("w_ffn_in", (DEPTH, 2, D, 2 * DFF))
        din("w_ffn_out", (DEPTH, 2, DFF, D))
        din("w_mix_in", (2, D, 4096))
        din("w_mix_out", (2, D, D))
        din("pool_w", (2, 4, 256, 256))
        din("cwT", (128, 48))
        din("pscT", (128, 16))
        din("invc", (128, 60))
        din("sconv", (2, 4, 2, 1024))
        din("spool", (2, 4, 15, 1024))
        din("w_qkv", (2, D, 2560))
        din("w_o", (2, D, D))
        din("sinkB", (128, 64))
        din("ident", (128, 128))
        din("biasP", (32, 128, 256))
        din("biasS", (32, 4, 8, 160))
        din("ck", (2, 4, 128, 256))
        din("cv", (2, 4, 128, 256))
        dout("kp", (2, 128, 256))
        dout("vp", (2, 128, 256))
        dout("ks", (2, 4, 128, 256))
        dout("vs", (2, 4, 128, 256))
        dout("yT", (128, NCH, T))
        dout("convp", (2, 2, 1024))
        dout("poolp", (2, 15, 1024))
        dout("convs", (2, 4, 2, 1024))
        dout("pools", (2, 4, 15, 1024))
        self.dr = dr

        with ExitStack() as st:
            XB = NCH * T * 4
            ARB = 92160
            self.NSLOT = 4
            SLOTB = 5632
            MISCB = 20992
            xt = st.enter_context(nc.sbuf_tensor("xT_sb", [128, XB // 4], F32))
            ar = st.enter_context(nc.sbuf_tensor("arena", [128, ARB // 4], F32))
            rg = st.enter_context(nc.sbuf_tensor("ring", [128, self.NSLOT * SLOTB // 4], F32))
            ms = st.enter_context(nc.sbuf_tensor("misc", [128, MISCB // 4], F32))
            ps = st.enter_context(nc.psum_tensor("ps", [128, 4, 2, 512], F32))
            self.ps = ps
            self.ring_i = 0

            self.X = Reg("x", xt, 0, F32, NCH, T, tile=TT)
            self.A_aT = Reg("ar", ar, 0, BF16, NFF, TP, tile=TT)
            self.A_sq = Reg("ar", ar, 0, BF16, NCH, TP, tile=TT)
            self.A_yT = Reg("ar", ar, 51600, F32, NCH, TP, tile=TT)
            self.A_hT = Reg("ar", ar, 51600, BF16, NCH, TP, tile=TT)
            self.B_hT = Reg("ar", ar, 0, BF16, NCH, T, tile=TT)
            self.B_yT = Reg("ar", ar, 0, F32, NCH, TP, tile=TT)
            self.B_mT = Reg("ar", ar, 38400, BF16, NCH, T, tile=TT)
            self.B_sq = Reg("ar", ar, 76800, BF16, 8, TP, tile=TT)
            self.B_Ev = Reg("ar", ar, 76800, F32, 1, 1212)
            self.B_yc = Reg("ar", ar, 81648, F32, 1, 1212)
            self.B_Eu = Reg("ar", ar, 76800, F32, 1, 1276)
            self.B_Es = Reg("ar", ar, 81904, F32, 1, 1276)
            self.B_dT = Reg("ar", ar, 87008, BF16, 2, T, tile=TT)
            self.slots = [Reg("ring", rg, i * SLOTB, BF16, 22, 128) for i in range(self.NSLOT)]
            o = 0
            self.M_g = Reg("misc", ms, o, F32, 1, DEPTH * 6 * NCH); o += DEPTH * 6 * NCH * 4
            self.M_rs = [Reg("misc", ms, o + k * 2400, F32, 1, TP, tile=TT) for k in range(2)]; o += 4800
            self.M_sg = [Reg("misc", ms, o + k * 1200, BF16, 1, TP, tile=TT) for k in range(2)]; o += 2400
            self.M_ones = Reg("misc", ms, o, BF16, 1, 128); o += 256
            self.M_cst = Reg("misc", ms, o, F32, 1, 8); o += 32
            self.M_cw = Reg("misc", ms, o, F32, 1, 48); o += 192
            self.M_psc = Reg("misc", ms, o, F32, 1, 16); o += 64
            self.M_invc = Reg("misc", ms, o, F32, 1, 60); o += 240
            self.M_t15 = Reg("misc", ms, o, F32, 1, 16); o += 64
            self.M_pw = Reg("misc", ms, o, BF16, 8, 256); o += 4096
            pwo = o - 4096
            self.C_KVf = Reg("misc", ms, pwo, F32, 4, 160)
            self.C_KcT = Reg("misc", ms, pwo + 2560, BF16, 4, 128)
            self.C_Vc = Reg("misc", ms, pwo + 3584, BF16, 1, 256)
            self.M_bias = [Reg("misc", ms, o + k * 1024, F32, 1, 256) for k in range(2)]; o += 2048
            self.M_S = Reg("misc", ms, o, F32, 1, 256); o += 1024
            self.M_P = Reg("misc", ms, o, BF16, 1, 256); o += 512
            self.M_PT = Reg("misc", ms, o, BF16, 2, 128); o += 512
            self.M_sm = Reg("misc", ms, o, F32, 1, 16); o += 64
            self.M_sink = Reg("misc", ms, o, F32, 1, 64); o += 256
            self.M_idb = Reg("misc", ms, o, BF16, 1, 128); o += 256
            self.M_idf = Reg("misc", ms, o, F32, 1, 128); o += 512
            self.M_stg = Reg("misc", ms, o, F32, 1, 512); o += 2048
            so_ = self.M_stg.off
            self.E_so = Reg("misc", ms, so_, F32, 1, 76)
            self.E_st = Reg("misc", ms, so_ + 304, F32, 1, 128)
            self.E_cs = Reg("misc", ms, so_ + 816, F32, 1, 128)
            self.C_hT = Reg("ar", ar, 0, BF16, NCH, T, tile=TT)
            self.C_attT = Reg("ar", ar, 0, BF16, NCH, T, tile=TT)
            self.C_qT = Reg("ar", ar, 38400, BF16, NCH, T, tile=TT)
            self.C_yT = Reg("ar", ar, 38400, F32, NCH, TP, tile=TT)
            self.C_kdT = Reg("ar", ar, 76800, BF16, 4, T, tile=TT)
            self.C_V = Reg("ar", ar, 86400, BF16, 11, 256)
            self.bias_i = 0
            mb = self.M_bias[0].off
            self.at_sets = []
            for k in range(2):
                ro = k * 8192
                self.at_sets.append((Reg("ring", rg, ro, F32, 1, 1024), Reg("ring", rg, ro + 4096, BF16, 1, 1024),
                                     Reg("ring", rg, ro + 6144, BF16, 1, 1024), [self.M_sm, self.M_t15][k]))
            self.at_bias = [Reg("ring", rg, 16384, F32, 1, 1024), Reg("misc", ms, mb, F32, 1, 1024)]
            self.at_i = 0
            self.at_diag = [Reg("ring", rg, 20480 + k * 1024, BF16, 1, 512) for k in range(2)]
            self.at_P3 = [self.at_sets[0][1], self.at_sets[1][1], Reg("misc", ms, pwo, BF16, 1, 1024)]
            self.at_p3 = 0
            assert o <= MISCB, o
            self.rs_i = 0
            self.sg_i = 0

            self.prologue()
            for i in range(L):
                if "ffn0" in self.phases:
                    self.ffn(i, 0)
                if "mix" in self.phases:
                    if i % 2 == 0:
                        self.mix_even(i)
                    else:
                        self.mix_odd(i)
                if "ffn1" in self.phases:
                    self.ffn(i, 1)
            self.epilogue()
            self.S.emit(nc, st)
        return nc

    def prologue(self):
        dr = self.dr
        X = self.X
        for c in range(NCH):
            self.dma(lambda e, c=c: e.dma_start(out=X.v[:, c, :], in_=dr["xT"][:, c, :]),
                     w=[X.iv(c)], sem="in%d" % (c % 4))
        for (reg, name) in ((self.M_g, "gT"), (self.M_cw, "cwT"), (self.M_psc, "pscT"), (self.M_invc, "invc"), (self.M_sink, "sinkB"), (self.M_idf, "ident")):
            self.dma(lambda e, reg=reg, name=name: e.dma_start(out=reg.v[:, 0, :], in_=dr[name][:, :]),
                     w=[reg.all()], sem="cst_" + name)
        O = self.M_ones
        self.dve(lambda e: e.memset(O.v[:, 0, :], 1.0), w=[O.all()])
        IB, IF = self.M_idb, self.M_idf
        self.dve(lambda e: e.tensor_copy(out=IB.v[:, 0, :], in_=IF.v[:, 0, :]), r=[IF.all()], w=[IB.all()])
        C = self.M_cst
        self.dve(lambda e: e.memset(C.v[:, 0, 0:1], EPS), w=[C.iv(0, 0, 1)])
        self.dve(lambda e: e.memset(C.v[:, 0, 1:2], 4.0 * EPS), w=[C.iv(0, 1, 2)])

    def epilogue(self):
        dr = self.dr
        X = self.X
        if self.pending is not None:
            for f in self.pending:
                f()
            self.pending = None
        for c in range(NCH):
            self.dma(lambda e, c=c: e.dma_start(out=dr["yT"][:, c, :], in_=X.v[:, c, :]),
                     r=[X.iv(c)], sem="out%d" % (c % 4))

    def stats(self, src, tiles, scale, bias_col, SQ):
        ps, C, O = self.ps, self.M_cst, self.M_ones
        rs = self.M_rs[self.rs_i]
        self.rs_i ^= 1
        G = SQ.nch
        nt = len(tiles)
        q = self.pair()
        for g0 in range(0, NCH, G):
            for cc in range(G):
                c = g0 + cc
                for t, st_ in enumerate(tiles):
                    self.act(lambda e, o=SQ.vt[:, cc, t, :], a=src.vt[:, c, st_, :]: e.activation(out=o, in_=a, func=AF.Square),
                             r=[src.iv(c, st_ * TT, st_ * TT + TT)], w=[SQ.iv(cc, t * TT, t * TT + TT)])
            for t in range(nt):
                for cc in range(G):
                    c = g0 + cc
                    self.pe(lambda e, o=ps[:, q, t, 0:TT], b=SQ.vt[:, cc, t, :], c=c: e.matmul(
                        o, lhsT=O.v[:, 0, :], rhs=b, start=(c == 0), stop=(c == NCH - 1)),
                        r=[O.all(), SQ.iv(cc, t * TT, t * TT + TT)], w=[self.ps_iv(q, t)])
        for t in range(nt):
            self.act(lambda e, o=rs.vt[:, 0, t, :], a=ps[:, q, t, 0:TT]: e.activation(
                out=o, in_=a, func=AF.Sqrt, scale=scale, bias=C.v[:, 0, bias_col:bias_col + 1]),
                r=[self.ps_iv(q, t), C.all()], w=[rs.iv(0, t * TT, t * TT + TT)])
            self.dve(lambda e, o=rs.vt[:, 0, t, :]: e.reciprocal(out=o, in_=o),
                     r=[rs.iv(0, t * TT, t * TT + TT)], w=[rs.iv(0, t * TT, t * TT + TT)])
        return rs

    def gcol(self, layer, k, c):
        return (layer * 6 + k) * NCH + c

    def prenorm(self, i, k, hT, h_t0, x_t0, SQ, resid=None):
        X, G = self.X, self.M_g
        if x_t0 in self.pref:
            rs = self.pref.pop(x_t0)
        else:
            if resid is not None:
                for f in resid:
                    f()
                resid = None
            self.rs_i = 0
            rs = self.stats(X, [x_t0, x_t0 + 1], 1.0 / D, 0, SQ)
        for c in range(NCH):
            if resid is not None and c % 2 == 0:
                resid[c // 2]()
            gc = self.gcol(i, k, c)
            self.dve(lambda e, o=hT.vt[:, c, h_t0:h_t0 + 2, :], a=X.vt[:, c, x_t0:x_t0 + 2, :], g=G.v[:, 0, gc:gc + 1],
                     b=rs.vt[:, 0, :, :]: e.scalar_tensor_tensor(out=o, in0=a, scalar=g, in1=b, op0=ALU.mult, op1=ALU.mult),
                     r=[X.iv(c, x_t0 * TT, x_t0 * TT + TP), G.all(), rs.all()], w=[hT.iv(c, h_t0 * TT, h_t0 * TT + TP)])
        if resid is not None:
            for c in range(NCH // 2, NCH):
                resid[c]()

    def out_proj(self, wsrc, nk, inT, in_t0, p, i, kpost, half, yT, SQ, hi_space=None, pref_next=False, defer=False):
        X, G, ps, C, O = self.X, self.M_g, self.ps, self.M_cst, self.M_ones
        t0 = 2 * p
        nx_t0 = 2 if p == 0 else 0
        do_pref = pref_next
        pieces = [(0, 22), (22, 43)] if nk == 43 else [(0, nk)]
        qy = self.pair()
        self.reserved.add(qy)
        qx = None
        if do_pref:
            qx = self.pair()
            self.reserved.add(qx)
        SGy, SGx = self.M_sg

        def statmm(c):
            for t in range(2):
                self.pe(lambda e, o=ps[:, qy, t, 0:TT], b=SGy.vt[:, 0, t, :], c=c: e.matmul(o, lhsT=O.v[:, 0, :], rhs=b, start=(c == 0), stop=(c == NCH - 1)),
                        r=[O.all(), SGy.iv(0, t * TT, t * TT + TT)], w=[self.ps_iv(qy, t)])
            if do_pref:
                for t in range(2):
                    self.pe(lambda e, o=ps[:, qx, t, 0:TT], b=SGx.vt[:, 0, t, :], c=c: e.matmul(o, lhsT=O.v[:, 0, :], rhs=b, start=(c == 0), stop=(c == NCH - 1)),
                            r=[O.all(), SGx.iv(0, t * TT, t * TT + TT)], w=[self.ps_iv(qx, t)])

        for oc in range(NCH):
            sl = []
            for (k0, k1) in pieces:
                k = self.ring_slot()
                SL = self.slots[k]
                self.wdma(lambda e, o=SL.v[:, 0:k1 - k0, :],
                          a=wsrc[k0 * 128:k1 * 128, oc * 128:(oc + 1) * 128].rearrange("(c p) n -> p c n", p=128):
                          e.dma_start(out=o, in_=a), w=[SL.all()], sem="ring%d" % k)
                sl.append((SL, k0, k1))
            q = self.pair()
            for t in range(2):
                for (SL, k0, k1) in sl:
                    for kc in range(k0, k1):
                        self.pe(lambda e, o=ps[:, q, t, 0:TT], a=SL.v[:, kc - k0, :], b=inT.vt[:, kc, in_t0 + t, :], kc=kc: e.matmul(
                            o, lhsT=a, rhs=b, start=(kc == 0), stop=(kc == nk - 1)),
                            r=[SL.iv(kc - k0), inT.iv(kc, (in_t0 + t) * TT, (in_t0 + t + 1) * TT)], w=[self.ps_iv(q, t)])
            if oc >= 1:
                statmm(oc - 1)
            self.act(lambda e, o=yT.vt[:, oc, :, :], a=ps[:, q, :, 0:TT]: e.activation(out=o, in_=a, func=AF.Copy),
                     r=[self.ps_iv(q)], w=[yT.iv(oc)])
            self.act(lambda e, o=SGy.vt[:, 0, :, :], a=yT.vt[:, oc, :, :]: e.activation(out=o, in_=a, func=AF.Square),
                     r=[yT.iv(oc)], w=[SGy.all()])
            if do_pref:
                self.act(lambda e, o=SGx.vt[:, 0, :, :], a=X.vt[:, oc, nx_t0:nx_t0 + 2, :]: e.activation(out=o, in_=a, func=AF.Square),
                         r=[X.iv(oc, nx_t0 * TT, nx_t0 * TT + TP)], w=[SGx.all()])
        statmm(NCH - 1)
        rs2 = self.M_rs[1]
        scale, bcol = (4.0 / D, 1) if half else (1.0 / D, 0)
        jobs = [(qy, rs2, scale, bcol)]
        if do_pref:
            jobs.append((qx, self.M_rs[0], 1.0 / D, 0))
        for (qq, rs, sc_, bc_) in jobs:
            for t in range(2):
                self.act(lambda e, o=rs.vt[:, 0, t, :], a=ps[:, qq, t, 0:TT], sc_=sc_, bc_=bc_: e.activation(
                    out=o, in_=a, func=AF.Sqrt, scale=sc_, bias=C.v[:, 0, bc_:bc_ + 1]),
                    r=[self.ps_iv(qq, t), C.all()], w=[rs.iv(0, t * TT, t * TT + TT)])
                self.dve(lambda e, o=rs.vt[:, 0, t, :]: e.reciprocal(out=o, in_=o),
                         r=[rs.iv(0, t * TT, t * TT + TT)], w=[rs.iv(0, t * TT, t * TT + TT)])
        self.reserved.discard(qy)
        if do_pref:
            self.reserved.discard(qx)
            self.pref[nx_t0] = self.M_rs[0]

        def resid_op(c):
            gc = self.gcol(i, kpost, c)
            self.dve(lambda e, o=yT.vt[:, c, :, :], g=G.v[:, 0, gc:gc + 1], b=rs2.vt[:, 0, :, :]: e.scalar_tensor_tensor(
                out=o, in0=o, scalar=g, in1=b, op0=ALU.mult, op1=ALU.mult),
                r=[yT.iv(c), G.all(), rs2.all()], w=[yT.iv(c)])
            self.dve(lambda e, o=X.vt[:, c, t0:t0 + 2, :], b=yT.vt[:, c, :, :]: e.tensor_tensor(out=o, in0=o, in1=b, op=ALU.add),
                     r=[X.iv(c, t0 * TT, t0 * TT + TP), yT.iv(c)], w=[X.iv(c, t0 * TT, t0 * TT + TP)])

        ops = [lambda c=c: resid_op(c) for c in range(NCH)]
        if defer:
            return ops
        for f in ops:
            f()
        return None

    def ffn(self, i, s):
        dr = self.dr
        aT, yT, hT, ps = self.A_aT, self.A_yT, self.A_hT, self.ps
        kpre = 0 if s == 0 else 4
        kpost = 1 if s == 0 else 5
        win = dr["w_ffn_in"][i, s]
        wout = dr["w_ffn_out"][i, s]
        resid, self.pending = self.pending, None
        for p in range(2):
            self.prenorm(i, kpre, hT, 0, 2 * p, self.A_sq, resid=resid)
            for j in range(NFF):
                sl = []
                for half in range(2):
                    k = self.ring_slot()
                    SL = self.slots[k]
                    col0 = half * DFF + j * 128
                    self.wdma(lambda e, o=SL.v[:, 0:16, :], a=win[:, col0:col0 + 128].rearrange("(c p) n -> p c n", p=128):
                              e.dma_start(out=o, in_=a), w=[SL.all()], sem="ring%d" % k)
                    sl.append(SL)
                qs = []
                for half in range(2):
                    q = self.pair()
                    qs.append(q)
                    SL = sl[half]
                    for t in range(2):
                        for c in range(NCH):
                            self.pe(lambda e, o=ps[:, q, t, 0:TT], a=SL.v[:, c, :], b=hT.vt[:, c, t, :], c=c: e.matmul(
                                o, lhsT=a, rhs=b, start=(c == 0), stop=(c == NCH - 1)),
                                r=[SL.iv(c), hT.iv(c, t * TT, t * TT + TT)], w=[self.ps_iv(q, t)])
                sg = self.M_sg[self.sg_i]
                self.sg_i ^= 1
                qg, qu = qs
                self.act(lambda e, o=sg.vt[:, 0, :, :], a=ps[:, qg, :, 0:TT]: e.activation(out=o, in_=a, func=AF.Silu),
                         r=[self.ps_iv(qg)], w=[sg.all()])
                self.dve(lambda e, o=aT.vt[:, j, :, :], a=ps[:, qu, :, 0:TT], b=sg.vt[:, 0, :, :]: e.tensor_tensor(
                    out=o, in0=a, in1=b, op=ALU.mult),
                    r=[self.ps_iv(qu), sg.all()], w=[aT.iv(j)])
            last = (i == self.layers - 1 and s == 1 and p == 1)
            r_ = self.out_proj(wout, NFF, aT, 0, p, i, kpost, True, yT, self.A_sq, pref_next=not last, defer=not last)
            if p == 0:
                resid = r_
            else:
                self.pending = r_

    def proj_chunk(self, wsrc_cols, hT):
        ps = self.ps
        k = self.ring_slot()
        SL = self.slots[k]
        self.wdma(lambda e, o=SL.v[:, 0:16, :], a=wsrc_cols.rearrange("(c p) n -> p c n", p=128): e.dma_start(out=o, in_=a),
                  w=[SL.all()], sem="ring%d" % k)
        qs = [self.pair(), self.pair()]
        for t in range(NT):
            q = qs[t // 2]
            for c in range(NCH):
                self.pe(lambda e, o=ps[:, q, t % 2, 0:TT], a=SL.v[:, c, :], b=hT.vt[:, c, t, :], c=c: e.matmul(
                    o, lhsT=a, rhs=b, start=(c == 0), stop=(c == NCH - 1)),
                    r=[SL.iv(c), hT.iv(c, t * TT, t * TT + TT)], w=[self.ps_iv(qs[t // 2], t % 2)])
        return qs

    def segs(self, qs):
        ps = self.ps
        out = []
        for t in range(3):
            out.append((ps[:, qs[t // 2], t % 2, 0:TT], self.ps_iv(qs[t // 2], t % 2), "p", t * TT, TT))
        out.append((ps[:, qs[1], 1, 0:R - 900], self.ps_iv(qs[1], 1), "p", 900, R - 900))
        out.append((ps[:, qs[1], 1, R - 900:TT].rearrange("p (s k) -> p s k", k=8), self.ps_iv(qs[1], 1), "s", 0, 0))
        return out

    @staticmethod
    def eseg(E, kind, r0, n, po, sb, ss, so):
        if kind == "p":
            return E.v[:, 0, po + r0:po + r0 + n], E.iv(0, po + r0, po + r0 + n)
        ap = E.v[:, 0, sb:sb + 4 * ss].rearrange("p (s k) -> p s k", k=ss)[:, :, so:so + 8]
        return ap, E.iv(0, sb, sb + 4 * ss)

    @staticmethod
    def nseg(M, c, kind, r0, n):
        if kind == "p":
            return M.v[:, c, r0:r0 + n], M.iv(c, r0, r0 + n)
        return M.v[:, c, R:T].rearrange("p (s k) -> p s k", k=8), M.iv(c, R, T)

    def mix_even(self, i):
        dr = self.dr
        j = i // 2
        hT, mT, yT, SQ = self.B_hT, self.B_mT, self.B_yT, self.B_sq
        Ev, yc, Eu, Es, dT = self.B_Ev, self.B_yc, self.B_Eu, self.B_Es, self.B_dT
        CW, PSC, INVC, T15, PW, ps = self.M_cw, self.M_psc, self.M_invc, self.M_t15, self.M_pw, self.ps
        SO, ST, CS, IF = self.E_so, self.E_st, self.E_cs, self.M_idf
        wmi = dr["w_mix_in"][j]
        for p in range(2):
            r_, self.pending = self.pending, None
            self.prenorm(i, 2, hT, 2 * p, 2 * p, SQ, resid=r_)
        self.wdma(lambda e, o=PW.v.rearrange("p (g k) n -> p g k n", k=2),
                  a=dr["pool_w"][j].rearrange("g (k p) n -> p g k n", p=128): e.dma_start(out=o, in_=a),
                  w=[PW.all()], sem="pw")
        EVW = 1210
        for c in range(8):
            ea = (2, 1170, 10, 2)
            qs = self.proj_chunk(wmi[:, c * 128:(c + 1) * 128], hT)
            self.dve(lambda e, o=Ev.v[:, 0, 0:2]: e.memset(o, 0.0), w=[Ev.iv(0, 0, 2)])
            self.dma(lambda e, o=CS.v[0:8, 0, :], a=dr["sconv"][j, :, :, c * 128:(c + 1) * 128].rearrange("s r p -> (s r) p"):
                     e.dma_start(out=o, in_=a), w=[CS.all()], sem="ctxc")
            qx = self.pair()
            xiv = self.ps_iv(qx, 0, 128)
            self.pe(lambda e, o=ps[:, qx, 0, 0:8], a=CS.v[0:8, 0, :], idn=IF.v[0:8, 0, 0:8]: e.transpose(o, a, idn), r=[CS.all(), IF.all()], w=[xiv])
            self.act(lambda e, o=Ev.v[:, 0, 1170:1210].rearrange("p (s k) -> p s k", k=10)[:, :, 0:2], a=ps[:, qx, 0, 0:8].rearrange("p (s k) -> p s k", k=2):
                     e.activation(out=o, in_=a, func=AF.Copy), r=[xiv], w=[Ev.iv(0, 1170, 1210)])
            for (pap, piv, kind, r0, n) in self.segs(qs):
                o, oiv = self.eseg(Ev, kind, r0, n, *ea)
                self.act(lambda e, o=o, a=pap: e.activation(out=o, in_=a, func=AF.Copy), r=[piv], w=[oiv])
            qs = self.proj_chunk(wmi[:, (8 + c) * 128:(9 + c) * 128], hT)
            for (pap, piv, kind, r0, n) in self.segs(qs):
                o, oiv = self.eseg(Ev, kind, r0, n, *ea)
                self.dve(lambda e, o=o, a=pap: e.tensor_tensor(out=o, in0=a, in1=o, op=ALU.mult), r=[piv, oiv], w=[oiv])
            self.dve(lambda e, o=SO.v[:, 0, 0:2], a=Ev.v[:, 0, 1168:1170]: e.tensor_copy(out=o, in_=a), r=[Ev.iv(0, 1168, 1170)], w=[SO.iv(0, 0, 2)])
            self.dve(lambda e, o=SO.v[:, 0, 2:10].rearrange("p (s k) -> p s k", k=2), a=Ev.v[:, 0, 1170:1210].rearrange("p (s k) -> p s k", k=10)[:, :, 8:10]:
                     e.tensor_copy(out=o, in_=a), r=[Ev.iv(0, 1170, 1210)], w=[SO.iv(0, 2, 10)])
            qx = self.pair()
            xiv = self.ps_iv(qx, 0, 128)
            self.pe(lambda e, o=ps[0:10, qx, 0, 0:128], a=SO.v[:, 0, 0:10], idn=IF.v[:, 0, :]: e.transpose(o, a, idn), r=[SO.iv(0, 0, 10), IF.all()], w=[xiv])
            self.act(lambda e, o=ST.v[0:10, 0, :], a=ps[0:10, qx, 0, 0:128]: e.activation(out=o, in_=a, func=AF.Copy), r=[xiv], w=[ST.all()])
            self.dma(lambda e, a=ST.v[0:2, 0, :], o=dr["convp"][j, :, c * 128:(c + 1) * 128]: e.dma_start(out=o, in_=a), r=[ST.all()], sem="stc")
            self.dma(lambda e, a=ST.v[2:10, 0, :], o=dr["convs"][j, :, :, c * 128:(c + 1) * 128].rearrange("s r p -> (s r) p"):
                     e.dma_start(out=o, in_=a), r=[ST.all()], sem="stc2")
            w0, w1, w2 = [CW.v[:, 0, (j * 3 + k) * 8 + c:(j * 3 + k) * 8 + c + 1] for k in range(3)]
            NY = 1208
            self.dve(lambda e, o=yc.v[:, 0, 0:NY], a=Ev.v[:, 0, 0:NY], w0=w0: e.tensor_scalar(out=o, in0=a, scalar1=w0, scalar2=None, op0=ALU.mult),
                     r=[Ev.iv(0, 0, NY), CW.all()], w=[yc.iv(0, 0, NY)])
            for k, wk in ((1, w1), (2, w2)):
                self.dve(lambda e, o=yc.v[:, 0, 0:NY], a=Ev.v[:, 0, k:k + NY], wk=wk: e.scalar_tensor_tensor(
                    out=o, in0=a, scalar=wk, in1=o, op0=ALU.mult, op1=ALU.add),
                    r=[Ev.iv(0, k, k + NY), CW.all(), yc.iv(0, 0, NY)], w=[yc.iv(0, 0, NY)])
            qs = self.proj_chunk(wmi[:, (16 + c) * 128:(17 + c) * 128], hT)
            for (pap, piv, kind, r0, n) in self.segs(qs):
                y_ap, y_iv = self.eseg(yc, kind, r0, n, 0, 1170, 10, 0)
                o, oiv = self.nseg(mT, c, kind, r0, n)
                self.dve(lambda e, o=o, a=pap, b=y_ap: e.tensor_tensor(out=o, in0=a, in1=b, op=ALU.mult), r=[piv, y_iv], w=[oiv])
        EUW = 1275
        for c in range(8):
            g, kc = c // 2, c % 2
            wnd = 2 << g
            ea = (15, 1183, 23, 15)
            qs = self.proj_chunk(wmi[:, (24 + c) * 128:(25 + c) * 128], hT)
            self.dve(lambda e, o=Eu.v[:, 0, 0:15]: e.memset(o, 0.0), w=[Eu.iv(0, 0, 15)])
            self.dma(lambda e, o=CS.v[0:60, 0, :], a=dr["spool"][j, :, :, c * 128:(c + 1) * 128].rearrange("s r p -> (s r) p"):
                     e.dma_start(out=o, in_=a), w=[CS.all()], sem="ctxp")
            qx = self.pair()
            xiv = self.ps_iv(qx, 0, 128)
            self.pe(lambda e, o=ps[:, qx, 0, 0:60], a=CS.v[0:60, 0, :], idn=IF.v[0:60, 0, 0:60]: e.transpose(o, a, idn), r=[CS.all(), IF.all()], w=[xiv])
            self.act(lambda e, o=Eu.v[:, 0, 1183:1275].rearrange("p (s k) -> p s k", k=23)[:, :, 0:15], a=ps[:, qx, 0, 0:60].rearrange("p (s k) -> p s k", k=15):
                     e.activation(out=o, in_=a, func=AF.Copy), r=[xiv], w=[Eu.iv(0, 1183, 1275)])
            for (pap, piv, kind, r0, n) in self.segs(qs):
                o, oiv = self.eseg(Eu, kind, r0, n, *ea)
                self.act(lambda e, o=o, a=pap: e.activation(out=o, in_=a, func=AF.Copy), r=[piv], w=[oiv])
            self.dve(lambda e, o=SO.v[:, 0, 0:15], a=Eu.v[:, 0, 1168:1183]: e.tensor_copy(out=o, in_=a), r=[Eu.iv(0, 1168, 1183)], w=[SO.iv(0, 0, 15)])
            self.dve(lambda e, o=SO.v[:, 0, 15:75].rearrange("p (s k) -> p s k", k=15), a=Eu.v[:, 0, 1183:1275].rearrange("p (s k) -> p s k", k=23)[:, :, 8:23]:
                     e.tensor_copy(out=o, in_=a), r=[Eu.iv(0, 1183, 1275)], w=[SO.iv(0, 15, 75)])
            qx = self.pair()
            xiv = self.ps_iv(qx, 0, 128)
            self.pe(lambda e, o=ps[0:75, qx, 0, 0:128], a=SO.v[:, 0, 0:75], idn=IF.v[:, 0, :]: e.transpose(o, a, idn), r=[SO.iv(0, 0, 75), IF.all()], w=[xiv])
            self.act(lambda e, o=ST.v[0:75, 0, :], a=ps[0:75, qx, 0, 0:128]: e.activation(out=o, in_=a, func=AF.Copy), r=[xiv], w=[ST.all()])
            self.dma(lambda e, a=ST.v[0:15, 0, :], o=dr["poolp"][j, :, c * 128:(c + 1) * 128]: e.dma_start(out=o, in_=a), r=[ST.all()], sem="stp")
            self.dma(lambda e, a=ST.v[15:75, 0, :], o=dr["pools"][j, :, :, c * 128:(c + 1) * 128].rearrange("s r p -> (s r) p"):
                     e.dma_start(out=o, in_=a), r=[ST.all()], sem="stp2")
            lo = wnd - 1
            n = EUW - lo
            self.dve(lambda e, o=Es.v[:, 0, lo:EUW], a=Eu.v[:, 0, lo:EUW], b=Eu.v[:, 0, lo - 1:EUW - 1]: e.tensor_tensor(out=o, in0=a, in1=b, op=ALU.add),
                     r=[Eu.iv(0, 0, EUW)], w=[Es.iv(0, lo, EUW)])
            for sh in range(2, wnd):
                self.dve(lambda e, o=Es.v[:, 0, lo:EUW], b=Eu.v[:, 0, lo - sh:EUW - sh]: e.tensor_tensor(out=o, in0=o, in1=b, op=ALU.add),
                         r=[Eu.iv(0, 0, EUW), Es.iv(0, lo, EUW)], w=[Es.iv(0, lo, EUW)])
            for (kind, r0, n_) in (("p", 0, R), ("s", 0, 0)):
                s_ap, s_iv = self.eseg(Es, kind, r0, n_, *ea)
                u_ap, u_iv = self.eseg(Eu, kind, r0, n_, *ea)
                o, oiv = self.nseg(dT, kc, kind, r0, n_)
                self.dve(lambda e, o=o, a=s_ap, b=u_ap, wnd=wnd: e.scalar_tensor_tensor(out=o, in0=a, scalar=1.0 / wnd, in1=b, op0=ALU.mult, op1=ALU.subtract),
                         r=[s_iv, u_iv], w=[oiv])
            self.dve(lambda e, o=T15.v[:, 0, 0:15], a=Es.v[:, 0, 15:30], b=INVC.v[:, 0, g * 15:(g + 1) * 15]: e.tensor_tensor(out=o, in0=a, in1=b, op=ALU.mult),
                     r=[Es.iv(0, 15, 30), INVC.all()], w=[T15.all()])
            self.dve(lambda e, o=dT.v[:, kc, 0:15], a=T15.v[:, 0, 0:15], b=Eu.v[:, 0, 15:30]: e.tensor_tensor(out=o, in0=a, in1=b, op=ALU.subtract),
                     r=[T15.all(), Eu.iv(0, 15, 30)], w=[dT.iv(kc, 0, 15)])
            if kc == 1:
                for m in range(2):
                    qs = [self.pair(), self.pair()]
                    for t in range(NT):
                        for k2 in range(2):
                            self.pe(lambda e, o=ps[:, qs[t // 2], t % 2, 0:TT], a=PW.v[:, g * 2 + k2, m * 128:(m + 1) * 128], b=dT.vt[:, k2, t, :], k2=k2: e.matmul(
                                o, lhsT=a, rhs=b, start=(k2 == 0), stop=(k2 == 1)),
                                r=[PW.all(), dT.iv(k2, t * TT, t * TT + TT)], w=[self.ps_iv(qs[t // 2], t % 2)])
                    ch = 8 + 2 * g + m
                    sc = PSC.v[:, 0, j * 8 + 2 * g + m:j * 8 + 2 * g + m + 1]
                    for hq in range(2):
                        self.act(lambda e, o=mT.vt[:, ch, 2 * hq:2 * hq + 2, :], a=ps[:, qs[hq], :, 0:TT], sc=sc: e.activation(out=o, in_=a, func=AF.Copy, scale=sc),
                                 r=[self.ps_iv(qs[hq]), PSC.all()], w=[mT.iv(ch, 2 * hq * TT, 2 * hq * TT + TP)])
        for p in range(2):
            r_ = self.out_proj(dr["w_mix_out"][j], NCH, mT, 2 * p, p, i, 3, False, yT, SQ, pref_next=(p == 1), defer=(p == 1))
            if p == 1:
                self.pending = r_


    def pool_op(self, fn, r=(), w=()):
        return self.S.add("pool", fn, r, w)

    def attn_front(self, g):
        ps = self.ps
        qT, SK = self.C_qT, self.M_sink
        if g.get("pre") is not None:
            g["pre"]()
        h0, j, qcols, nq, c0, c1, kparts, B = g["h0"], g["j"], g["qcols"], g["nq"], g["c0"], g["c1"], g["kparts"], g["B"]
        S, P_, PT, SM = self.at_sets[self.at_i]
        P = self.at_P3[self.at_p3]
        self.at_p3 = (self.at_p3 + 1) % 3
        g["Dg"] = self.at_diag[self.at_i]
        self.at_i ^= 1
        g["set"] = (S, P, PT, SM)
        n = c1 - c0
        q1 = self.pair()
        Sps = ps[:, q1].rearrange("p t (h n) -> p (t h) n", n=256)
        s_iv = self.ps_iv(q1)
        HS = (0, 2, 1, 3)
        for sl_ in range(4):
            h = h0 + HS[sl_]
            base = 64 * (h % 2)
            ch = h // 2
            for (reg, kvi, lo, nn, soff) in kparts:
                self.pe(lambda e, o=Sps[0:nq, sl_, soff:soff + nn], a=qT.v[base:base + 64, ch, qcols:qcols + nq],
                        b=reg.v[base:base + 64, kvi, lo:lo + nn]: e.matmul(o, lhsT=a, rhs=b, start=True, stop=True),
                        r=[qT.iv(ch, qcols, qcols + nq), reg.iv(kvi, lo, lo + nn)], w=[s_iv])
        Sv = S.v.rearrange("p o (h n) -> p (o h) n", n=256)
        Bv = B.v.rearrange("p o (h n) -> p (o h) n", n=256)
        Sw, Bw = Sv[0:nq, :, c0:c1], Bv[0:nq, :, c0:c1]
        mx, den, esk, rden = [SM.v[0:nq, 0, 4 * k:4 * k + 4] for k in range(4)]
        negm = rden
        sk = SK.v[0:nq, 0, j * 32 + h0:j * 32 + h0 + 4]
        X_ = mybir.AxisListType.X
        self.dve(lambda e, o=Sw, a=Sps[0:nq, :, c0:c1], b=Bw: e.scalar_tensor_tensor(out=o, in0=a, scalar=0.125, in1=b, op0=ALU.mult, op1=ALU.add),
                 r=[s_iv, B.all()], w=[S.all()])
        self.dve(lambda e, o=mx, a=Sw: e.tensor_reduce(out=o, in_=a, axis=X_, op=ALU.max), r=[S.all()], w=[SM.iv(0, 0, 4)])
        self.dve(lambda e, o=mx, b=sk: e.tensor_tensor(out=o, in0=o, in1=b, op=ALU.max), r=[SM.iv(0, 0, 4), SK.all()], w=[SM.iv(0, 0, 4)])
        self.dve(lambda e, o=negm, a=mx: e.tensor_scalar(out=o, in0=a, scalar1=-1.0, scalar2=None, op0=ALU.mult), r=[SM.iv(0, 0, 4)], w=[SM.iv(0, 12, 16)])
        self.dve(lambda e, o=esk, a=sk, b=mx: e.tensor_tensor(out=o, in0=a, in1=b, op=ALU.subtract), r=[SM.iv(0, 0, 4), SK.all()], w=[SM.iv(0, 8, 12)])
        Pv = P.v.rearrange("p o (h n) -> p (o h) n", n=256)
        for sl_ in range(4):
            self.act(lambda e, o=Pv[0:nq, sl_, c0:c1], a=Sv[0:nq, sl_, c0:c1], nb=SM.v[0:nq, 0, 12 + sl_:13 + sl_], d=SM.v[0:nq, 0, 4 + sl_:5 + sl_]: e.activation(
                out=o, in_=a, func=AF.Exp, bias=nb, scale=1.0, accum_out=d),
                r=[S.iv(0, sl_ * 256, sl_ * 256 + 256), SM.iv(0, 12 + sl_, 13 + sl_)], w=[P.iv(0, sl_ * 256, sl_ * 256 + 256), SM.iv(0, 4 + sl_, 5 + sl_)])
        self.act(lambda e, o=esk: e.activation(out=o, in_=o, func=AF.Exp), r=[SM.iv(0, 8, 12)], w=[SM.iv(0, 8, 12)])
        return g

    def attn_back1(self, g):
        IB = self.M_idb
        nq = g["nq"]
        S, P, PT, SM = g["set"]
        Dg = g["Dg"]
        mx, den, esk, rden = [SM.v[0:nq, 0, 4 * k:4 * k + 4] for k in range(4)]
        self.dve(lambda e, o=den, b=esk: e.tensor_tensor(out=o, in0=o, in1=b, op=ALU.add), r=[SM.iv(0, 4, 12)], w=[SM.iv(0, 4, 8)])
        self.dve(lambda e, o=rden, a=den: e.reciprocal(out=o, in_=a), r=[SM.iv(0, 4, 8)], w=[SM.iv(0, 12, 16)])
        Dv = Dg.v.rearrange("p o (h n) -> p (o h) n", n=128)
        self.pool_op(lambda e, o=Dv[0:nq, :, 0:nq], a=IB.v[0:nq, 0, 0:nq].unsqueeze(1).broadcast_to([nq, 4, nq]),
                     b=rden.unsqueeze(2).broadcast_to([nq, 4, nq]): e.tensor_tensor(out=o, in0=a, in1=b, op=ALU.mult),
                     r=[IB.all(), SM.iv(0, 12, 16)], w=[Dg.all()])

    def attn_back1c(self, g):
        ps = self.ps
        nq, vparts = g["nq"], g["vparts"]
        S, P, PT, SM = g["set"]
        Dg = g["Dg"]
        Pv = P.v.rearrange("p o (h n) -> p (o h) n", n=256)
        Dv = Dg.v.rearrange("p o (h n) -> p (o h) n", n=128)
        q2 = self.pair()
        PTps = ps[:, q2].rearrange("p t (b n) -> p (t b) n", n=128)
        pt_iv = self.ps_iv(q2)
        for hh in range(4):
            for (vfn, viv, nn, soff, bi) in vparts:
                self.pe(lambda e, o=PTps[0:nn, bi * 4 + hh, 0:nq], a=Pv[0:nq, hh, soff:soff + nn], b=Dv[0:nq, hh, 0:nq]: e.matmul(
                    o, lhsT=a, rhs=b, start=True, stop=True), r=[P.all(), Dg.all()], w=[pt_iv])
        PTv = PT.v.rearrange("p o (b n) -> p (o b) n", n=128)
        self.act(lambda e, o=PTv[:, :, 0:nq], a=PTps[:, :, 0:nq]: e.activation(out=o, in_=a, func=AF.Copy), r=[pt_iv], w=[PT.all()])

    def attn_back2(self, g):
        ps, attT = self.ps, self.C_attT
        h0, qcols, nq, vparts = g["h0"], g["qcols"], g["nq"], g["vparts"]
        S, P, PT, SM = g["set"]
        q3 = self.pair()
        kv = h0 // 8
        ch0 = h0 // 2
        PTv = PT.v.rearrange("p o (b n) -> p (o b) n", n=128)
        Ops = ps[:, q3, 0, :].rearrange("p (h n) -> p h n", n=128)
        o_iv = ("ps", q3 * 4096, q3 * 4096 + 2048)
        nb = len(vparts)
        for k_, (vfn, viv, nn, soff, bi) in enumerate(vparts):
            self.pe(lambda e, o=Ops[0:64, :, 0:nq], a=vfn(kv), b=PTv[0:nn, bi * 4:bi * 4 + 4, 0:nq], k_=k_: e.matmul(
                o, lhsT=a, rhs=b, start=(k_ == 0), stop=(k_ == nb - 1)), r=[viv, PT.all()], w=[o_iv])
        self.act(lambda e, o=attT.v[0:64, ch0:ch0 + 2, qcols:qcols + nq], a=Ops[0:64, 0:2, 0:nq]: e.activation(out=o, in_=a, func=AF.Copy),
                 r=[o_iv], w=attT.ivs(ch0, ch0 + 2, qcols, qcols + nq))
        self.act(lambda e, o=attT.v[64:128, ch0:ch0 + 2, qcols:qcols + nq], a=Ops[0:64, 2:4, 0:nq]: e.activation(out=o, in_=a, func=AF.Copy),
                 r=[o_iv], w=attT.ivs(ch0, ch0 + 2, qcols, qcols + nq))

    def attn_pipeline(self, groups):
        n = len(groups)
        for k in range(n + 5):
            if 0 <= k - 5 < n:
                self.attn_back2(groups[k - 5])
            if 0 <= k - 3 < n:
                self.attn_back1c(groups[k - 3])
            if k < n:
                self.attn_front(groups[k])
            if 0 <= k - 1 < n:
                self.attn_back1(groups[k - 1])

    def mix_odd(self, i):
        dr = self.dr
        j = i // 2
        ps = self.ps
        hT, qT, attT, yT, kdT, V, SQ = self.C_hT, self.C_qT, self.C_attT, self.C_yT, self.C_kdT, self.C_V, self.B_sq
        KVf, KcT, Vc, STG, IB, IF = self.C_KVf, self.C_KcT, self.C_Vc, self.M_stg, self.M_idb, self.M_idf
        wq = dr["w_qkv"][j]
        for p in range(2):
            r_, self.pending = self.pending, None
            self.prenorm(i, 2, hT, 2 * p, 2 * p, SQ, resid=r_)
        for cp in range(2):
            qs = self.proj_chunk(wq[:, 2048 + cp * 128:2048 + (cp + 1) * 128], hT)
            for par in range(2):
                kv = 2 * cp + par
                for hq in range(2):
                    for dst in range(2):
                        self.act(lambda e, o=kdT.vt[dst * 64:dst * 64 + 64, kv, 2 * hq:2 * hq + 2, :], a=ps[par * 64:par * 64 + 64, qs[hq], :, 0:TT]:
                                 e.activation(out=o, in_=a, func=AF.Copy),
                                 r=[self.ps_iv(qs[hq])], w=[kdT.iv(kv, 2 * hq * TT, 2 * hq * TT + TP)])
            self.dve(lambda e, o=KVf.v[:, cp, :], a=ps[:, qs[1], 1, 140:300]: e.tensor_copy(out=o, in_=a),
                     r=[self.ps_iv(qs[1], 1)], w=[KVf.iv(cp)])
        for cp in range(2):
            qs = self.proj_chunk(wq[:, 2304 + cp * 128:2304 + (cp + 1) * 128], hT)
            for hq in range(2):
                self.act(lambda e, o=qT.vt[:, 14 + cp, 2 * hq:2 * hq + 2, :], a=ps[:, qs[hq], :, 0:TT]: e.activation(out=o, in_=a, func=AF.Copy),
                         r=[self.ps_iv(qs[hq])], w=[qT.iv(14 + cp, 2 * hq * TT, 2 * hq * TT + TP)])
            self.dve(lambda e, o=KVf.v[:, 2 + cp, :], a=ps[:, qs[1], 1, 140:300]: e.tensor_copy(out=o, in_=a),
                     r=[self.ps_iv(qs[1], 1)], w=[KVf.iv(2 + cp)])
        blocks = [(b * 128, 128) for b in range(9)] + [(1152, 16), (1168, 32)]
        for b, (k0, n) in enumerate(blocks):
            q = self.pair()
            psb = ps[:, q, 0, 0:128].bitcast(BF16)
            piv = ("ps", q * 4096, q * 4096 + 512)
            for cp in range(2):
                self.pe(lambda e, o=psb[0:n, cp * 128:(cp + 1) * 128], a=qT.v[:, 14 + cp, k0:k0 + n], idn=IB.v[:, 0, :]: e.transpose(o, a, idn),
                        r=[qT.iv(14 + cp, k0, k0 + n), IB.all()], w=[piv])
            self.act(lambda e, o=V.v[0:n, b, :], a=psb[0:n, 0:256]: e.activation(out=o, in_=a, func=AF.Copy), r=[piv], w=[V.iv(b)])
        for (c0, n, which) in ((0, 128, "p"), (128, 32, "s")):
            q = self.pair()
            piv = self.ps_iv(q, 0, 512)
            for ch in range(4):
                self.pe(lambda e, o=ps[0:n, q, 0, ch * 128:(ch + 1) * 128], a=KVf.v[:, ch, c0:c0 + n], idn=IF.v[:, 0, :]: e.transpose(o, a, idn),
                        r=[KVf.iv(ch, c0, c0 + n), IF.all()], w=[piv])
            self.act(lambda e, o=STG.v[0:n, 0, :], a=ps[0:n, q, 0, 0:512]: e.activation(out=o, in_=a, func=AF.Copy), r=[piv], w=[STG.all()])
            if which == "p":
                self.dma(lambda e, o=dr["kp"][j], a=STG.v[:, 0, 0:256]: e.dma_start(out=o, in_=a), r=[STG.all()], sem="okp")
                self.dma(lambda e, o=dr["vp"][j], a=STG.v[:, 0, 256:512]: e.dma_start(out=o, in_=a), r=[STG.all()], sem="ovp")
            else:
                for sq in range(4):
                    self.dma(lambda e, o=dr["ks"][j, sq, 120:128, :], a=STG.v[8 * sq:8 * sq + 8, 0, 0:256]: e.dma_start(out=o, in_=a), r=[STG.all()], sem="oks%d" % sq)
                    self.dma(lambda e, o=dr["vs"][j, sq, 120:128, :], a=STG.v[8 * sq:8 * sq + 8, 0, 256:512]: e.dma_start(out=o, in_=a), r=[STG.all()], sem="ovs%d" % sq)
                    self.dma(lambda e, o=dr["ks"][j, sq, 0:120, :], a=dr["ck"][j, sq, 8:128, :]: e.dma_start(out=o, in_=a), sem="cks%d" % sq)
                    self.dma(lambda e, o=dr["vs"][j, sq, 0:120, :], a=dr["cv"][j, sq, 8:128, :]: e.dma_start(out=o, in_=a), sem="cvs%d" % sq)
        for c in range(NCH):
            qs = self.proj_chunk(wq[:, c * 128:(c + 1) * 128], hT)
            for hq in range(2):
                self.act(lambda e, o=qT.vt[:, c, 2 * hq:2 * hq + 2, :], a=ps[:, qs[hq], :, 0:TT]: e.activation(out=o, in_=a, func=AF.Copy),
                         r=[self.ps_iv(qs[hq])], w=[qT.iv(c, 2 * hq * TT, 2 * hq * TT + TP)])
        groups = []
        for hg in range(8):
            h0 = 4 * hg
            B = self.at_bias[hg % 2]

            def pre_p(B=B, h0=h0, hg=hg):
                self.dma(lambda e, o=B.v.rearrange("p o (h n) -> p (o h) n", n=256), a=dr["biasP"][h0:h0 + 4].rearrange("h q s -> q h s"):
                         e.dma_start(out=o, in_=a), w=[B.all()], sem="bias%d" % (hg % 2))
            for qb in range(10):
                q0 = qb * 128
                nq = 128 if qb < 9 else 16
                if qb == 0:
                    kparts = [(kdT, h0 // 8, 0, nq, 128)]
                    vparts = [(lambda kv: V.v[0:128, 0, kv * 64:(kv + 1) * 64], V.iv(0), nq, 128, 1)]
                    c0, c1 = 128, 128 + nq
                else:
                    kparts = [(kdT, h0 // 8, q0 - 128, 128 + nq, 0)]
                    vparts = [(lambda kv, qb=qb: V.v[0:128, qb - 1, kv * 64:(kv + 1) * 64], V.iv(qb - 1), 128, 0, 0),
                              (lambda kv, qb=qb, nq=nq: V.v[0:nq, qb, kv * 64:(kv + 1) * 64], V.iv(qb), nq, 128, 1)]
                    c0, c1 = 0, 128 + nq
                groups.append(dict(h0=h0, j=j, qcols=q0, nq=nq, c0=c0, c1=c1, kparts=kparts, vparts=vparts, B=B,
                                   pre=pre_p if qb == 0 else None))
        for sq in range(4):
            def pre_seq(sq=sq):
                self.dma(lambda e, o=STG.v[:, 0, 0:256], a=dr["ck"][j, sq]: e.dma_start(out=o, in_=a), w=[STG.iv(0, 0, 256)], sem="ckl")
                self.wdma(lambda e, o=Vc.v[:, 0, :], a=dr["cv"][j, sq]: e.dma_start(out=o, in_=a), w=[Vc.all()], sem="cvl")
                q = self.pair()
                piv = self.ps_iv(q, 0, 512)
                for kv in range(4):
                    self.pe(lambda e, o=ps[0:64, q, 0, kv * 128:(kv + 1) * 128], a=STG.v[:, 0, kv * 64:(kv + 1) * 64], idn=IF.v[:, 0, :]: e.transpose(o, a, idn),
                            r=[STG.iv(0, 0, 256), IF.all()], w=[piv])
                for dst in range(2):
                    self.act(lambda e, o=KcT.v[dst * 64:dst * 64 + 64, :, :], a=ps[0:64, q, 0, 0:512].rearrange("p (k n) -> p k n", k=4):
                             e.activation(out=o, in_=a, func=AF.Copy), r=[piv], w=[KcT.all()])
            for hg in range(8):
                h0 = 4 * hg
                B = self.at_bias[hg % 2]

                def pre_s(B=B, h0=h0, hg=hg, sq=sq, first=(hg == 0)):
                    if first:
                        pre_seq(sq)
                    self.dma(lambda e, o=B.v.rearrange("p o (h n) -> p (o h) n", n=256)[0:8, :, 0:160], a=dr["biasS"][h0:h0 + 4, sq].rearrange("h q s -> q h s"):
                             e.dma_start(out=o, in_=a), w=[B.all()], sem="bias%d" % (hg % 2))
                q0 = R + 8 * sq
                kparts = [(KcT, h0 // 8, 0, 128, 0), (kdT, h0 // 8, R, 32, 128)]
                vparts = [(lambda kv: Vc.v[:, 0, kv * 64:(kv + 1) * 64], Vc.all(), 128, 0, 0),
                          (lambda kv: V.v[0:32, 10, kv * 64:(kv + 1) * 64], V.iv(10), 32, 128, 1)]
                groups.append(dict(h0=h0, j=j, qcols=q0, nq=8, c0=0, c1=160, kparts=kparts, vparts=vparts, B=B, pre=pre_s))
            self.attn_pipeline(groups)
            groups = []
        for p in range(2):
            self.out_proj(dr["w_o"][j], NCH, attT, 2 * p, p, i, 3, False, yT, SQ, pref_next=(p == 1))


_NC_CACHE = {}


def get_nc(layers=DEPTH, phases=("ffn0", "mix", "ffn1")):
    key = (layers, tuple(phases))
    if key not in _NC_CACHE:
        _NC_CACHE[key] = Builder(layers, phases).build()
    return _NC_CACHE[key]


def _t5_bucket(dist):
    n = np.maximum(dist, 0)
    ratio = np.log(np.maximum(n, 1).astype(np.float32) / np.float32(16)) / np.float32(math.log(128 / 16))
    large = np.minimum(16 + (ratio * np.float32(16)).astype(np.int32), 31)
    return np.where(n < 16, n, large)


def make_in_maps(inp, ncores=8):
    f = lambda k: np.asarray(inp[k], np.float32)
    xp, xs, g = f("x_prompt"), f("x_sample"), f("norm_g")
    gT = np.ascontiguousarray(g.reshape(DEPTH * 6, NCH, 128).transpose(2, 0, 1).reshape(128, -1))
    cw = f("conv_w")
    cwT = np.ascontiguousarray(cw.reshape(2, 3, 8, 128).transpose(3, 0, 1, 2).reshape(128, 48))
    psc = f("pool_scale")
    pscT = np.ascontiguousarray(psc.reshape(2, 8, 128).transpose(2, 0, 1).reshape(128, 16))
    sinkB = np.ascontiguousarray(np.broadcast_to(f("attn_sinks").reshape(1, 64), (128, 64)))
    ident = np.eye(128, dtype=np.float32)
    rb = f("rel_bias")
    qi = np.arange(128)[:, None]
    sj = np.arange(256)[None, :]
    dist = 128 + qi - sj
    valid = (dist >= 0) & (dist <= 128)
    bk = _t5_bucket(dist)
    biasP = np.where(valid[None], rb[bk].transpose(2, 0, 1), np.float32(NEG)).astype(np.float32)
    biasS = np.full((32, 4, 8, 160), NEG, np.float32)
    q8 = np.arange(8)[:, None]
    sc = np.arange(128)[None, :]
    dc = q8 + 128 - sc
    vc = (dc >= 0) & (dc <= 128)
    bc = np.where(vc[None], rb[_t5_bucket(dc)].transpose(2, 0, 1), np.float32(NEG))
    i8 = np.arange(8)[None, :]
    dn = q8 - i8
    vn = dn >= 0
    bn = np.where(vn[None], rb[_t5_bucket(dn)].transpose(2, 0, 1), np.float32(NEG))
    for s_ in range(4):
        biasS[:, s_, :, 0:128] = bc
        biasS[:, s_, :, 128 + 8 * s_:136 + 8 * s_] = bn
    perm = np.array([g4 * 4 + o_ for g4 in range(8) for o_ in (0, 2, 1, 3)])
    biasP = biasP[perm]
    biasS = np.ascontiguousarray(biasS[perm])
    sinkB = np.ascontiguousarray(sinkB.reshape(128, 2, 32)[:, :, perm].reshape(128, 64))
    shared = {"gT": gT, "cwT": cwT, "pscT": pscT, "sinkB": sinkB, "ident": ident, "biasP": np.ascontiguousarray(biasP),
              "biasS": biasS}
    for k_ in ("w_ffn_in", "w_ffn_out", "w_mix_in", "w_mix_out", "pool_w", "w_qkv", "w_o"):
        shared[k_] = f(k_)
    sconv, spool = f("state_conv"), f("state_pool")
    ck = f("cache_k").reshape(2, 32, 128, 256)
    cv = f("cache_v").reshape(2, 32, 128, 256)
    maps = []
    for core in range(ncores):
        b, half = core // 2, core % 2
        r0 = 0 if half == 0 else HALO0
        rows = np.concatenate([xp[b, r0:r0 + R], xs[4 * core:4 * core + 4].reshape(NS, D)], axis=0)
        xT = np.ascontiguousarray(rows.reshape(T, NCH, 128).transpose(2, 1, 0))
        invc = np.zeros((128, 4, 15), np.float32)
        for gi, wn in enumerate((2, 4, 8, 16)):
            cnt = np.minimum(wn, r0 + np.arange(15) + 1).astype(np.float32)
            invc[:, gi, :] = (np.float32(1.0) / cnt)[None, :]
        m = dict(shared)
        m.update({"xT": xT, "invc": np.ascontiguousarray(invc.reshape(128, 60)),
                  "sconv": np.ascontiguousarray(sconv[:, 4 * core:4 * core + 4]),
                  "spool": np.ascontiguousarray(spool[:, 4 * core:4 * core + 4]),
                  "ck": np.ascontiguousarray(ck[:, 4 * core:4 * core + 4]),
                  "cv": np.ascontiguousarray(cv[:, 4 * core:4 * core + 4])})
        maps.append(m)
    return maps


def assemble(results):
    yp = np.zeros((4, 2048, D), np.float32)
    ys = np.zeros((32, 8, D), np.float32)
    convp = np.zeros((2, 4, 2, 1024), np.float32)
    poolp = np.zeros((2, 4, 15, 1024), np.float32)
    kp = np.zeros((2, 4, 128, 4, 64), np.float32)
    vp = np.zeros((2, 4, 128, 4, 64), np.float32)
    convs = np.zeros((2, 32, 2, 1024), np.float32)
    pools = np.zeros((2, 32, 15, 1024), np.float32)
    ks = np.zeros((2, 32, 128, 4, 64), np.float32)
    vs = np.zeros((2, 32, 128, 4, 64), np.float32)
    for core, res in enumerate(results):
        rows = np.asarray(res["yT"]).transpose(2, 1, 0).reshape(T, D)
        b, half = core // 2, core % 2
        if half == 0:
            yp[b, 0:R] = rows[0:R]
        else:
            yp[b, R:2048] = rows[2 * R - 2048:R]
            convp[:, b] = np.asarray(res["convp"])
            poolp[:, b] = np.asarray(res["poolp"])
            kp[:, b] = np.asarray(res["kp"]).reshape(2, 128, 4, 64)
            vp[:, b] = np.asarray(res["vp"]).reshape(2, 128, 4, 64)
        ys[4 * core:4 * core + 4] = rows[R:].reshape(4, 8, D)
        convs[:, 4 * core:4 * core + 4] = np.asarray(res["convs"])
        pools[:, 4 * core:4 * core + 4] = np.asarray(res["pools"])
        ks[:, 4 * core:4 * core + 4] = np.asarray(res["ks"]).reshape(2, 4, 128, 4, 64)
        vs[:, 4 * core:4 * core + 4] = np.asarray(res["vs"]).reshape(2, 4, 128, 4, 64)
    return (yp, ys, convp, poolp, kp, vp, convs, pools, ks, vs)


def kernel(**inputs):
    nc = get_nc()
    maps = make_in_maps(inputs)
    res = run_bass_kernel_spmd(nc, maps, core_ids=list(range(8)))
    return assemble(res.results)
```
